# Optimizing a Trainium2 kernel written in Bass

```python
import jax, jax.numpy as jnp
from jax import lax
import numpy as np

D_MODEL = 4096
BATCH = 1
SEQ = 16384
DEPTH = 4

CHUNK = 64
N_MIXERS = 2
N_MEM = 256
MIX_WIDTH = D_MODEL
MEM_HEADS = 4
MEM_WIDTH = MIX_WIDTH // 4
MEM_HEAD_DIM = MEM_WIDTH // MEM_HEADS
BRANCH_WIDTH = MIX_WIDTH - MEM_WIDTH
CONV_WIDTH = 3
GMLP_BLOCK = 128
GMLP_GROUPS = 8
GMLP_GROUP_DIM = BRANCH_WIDTH // GMLP_GROUPS
CONV_IN_WIDTH = 3 * BRANCH_WIDTH + MEM_WIDTH + MIX_WIDTH
GMLP_IN_WIDTH = 2 * BRANCH_WIDTH + MEM_WIDTH + MIX_WIDTH
N_CONV_LAYERS = (DEPTH + 1) // 2
N_GMLP_LAYERS = DEPTH // 2
RMS_EPS = 1e-6
LN_EPS = 1e-5

kernel_name = "hybrid_conv_gmlp_memory_trunk"


def rms_norm(x, g):
    xf = x.astype(jnp.float32)
    y = xf * lax.rsqrt(jnp.mean(xf * xf, axis=-1, keepdims=True) + RMS_EPS)
    return (y * g.astype(jnp.float32)).astype(x.dtype)


def layer_norm(x, g, b):
    xf = x.astype(jnp.float32)
    mu = jnp.mean(xf, axis=-1, keepdims=True)
    xc = xf - mu
    y = xc * lax.rsqrt(jnp.mean(xc * xc, axis=-1, keepdims=True) + LN_EPS)
    return (y * g.astype(jnp.float32) + b.astype(jnp.float32)).astype(x.dtype)


def memory_attention(q, mem, mem_g, w_kv):
    b, s, _ = q.shape
    m = mem.shape[1]
    kv = rms_norm(mem, mem_g) @ w_kv
    k, v = jnp.split(kv, 2, axis=-1)
    q = q.reshape(b, s, MEM_HEADS, MEM_HEAD_DIM)
    k = k.reshape(b, m, MEM_HEADS, MEM_HEAD_DIM)
    v = v.reshape(b, m, MEM_HEADS, MEM_HEAD_DIM)
    scores = jnp.einsum('bshd,bmhd->bhsm', q, k).astype(jnp.float32) * (MEM_HEAD_DIM ** -0.5)
    p = jax.nn.softmax(scores, axis=-1).astype(v.dtype)
    o = jnp.einsum('bhsm,bmhd->bshd', p, v)
    return o.reshape(b, s, MEM_WIDTH)


def causal_short_conv(h, w):
    s = h.shape[1]
    hp = jnp.pad(h, ((0, 0), (CONV_WIDTH - 1, 0), (0, 0)))
    out = hp[:, 0:s] * w[0]
    for k in range(1, CONV_WIDTH):
        out = out + hp[:, k:k + s] * w[k]
    return out


def conv_branch(paths, w_conv):
    b_gate, c_gate, h = jnp.split(paths, 3, axis=-1)
    return b_gate * causal_short_conv(c_gate * h, w_conv)


def gmlp_branch(paths, ln_g, ln_b, w_s, b_s):
    z = jax.nn.gelu(paths)
    u, v = jnp.split(z, 2, axis=-1)
    v = layer_norm(v, ln_g, ln_b)
    b, s, _ = v.shape
    n_blocks = s // GMLP_BLOCK
    v = v.reshape(b, n_blocks, GMLP_BLOCK, GMLP_GROUPS, GMLP_GROUP_DIM)
    mask = jnp.tril(jnp.ones((GMLP_BLOCK, GMLP_BLOCK), dtype=bool))
    w = jnp.where(mask[None], w_s, jnp.zeros_like(w_s))
    f = jnp.einsum('gts,bnsgd->bntgd', w, v) + jnp.transpose(b_s)[None, None, :, :, None]
    return u * f.reshape(b, s, BRANCH_WIDTH)


def setup_inputs(seed: int = 0) -> dict:
    key = jax.random.key(seed)
    ks = jax.random.split(key, 16)
    f32 = jnp.float32
    x = jax.random.normal(ks[0], (BATCH, SEQ, D_MODEL), f32)
    mem = jax.random.normal(ks[1], (BATCH, N_MEM, D_MODEL), f32)
    pre_norm_g = 1.0 + 0.1 * jax.random.normal(ks[2], (DEPTH, D_MODEL), f32)
    post_norm_g = 1.0 + 0.1 * jax.random.normal(ks[3], (DEPTH, D_MODEL), f32)
    mem_norm_g = 1.0 + 0.1 * jax.random.normal(ks[4], (DEPTH, D_MODEL), f32)
    w_mem_kv = jax.random.normal(ks[5], (DEPTH, D_MODEL, 2 * MEM_WIDTH), f32) * D_MODEL ** -0.5
    w_out = jax.random.normal(ks[6], (DEPTH, MIX_WIDTH, D_MODEL), f32) * MIX_WIDTH ** -0.5
    conv_w_in = jax.random.normal(ks[7], (N_CONV_LAYERS, D_MODEL, CONV_IN_WIDTH), f32) * D_MODEL ** -0.5
    conv_w = jax.random.normal(ks[8], (N_CONV_LAYERS, CONV_WIDTH, BRANCH_WIDTH), f32) * CONV_WIDTH ** -0.5
    gmlp_w_in = jax.random.normal(ks[9], (N_GMLP_LAYERS, D_MODEL, GMLP_IN_WIDTH), f32) * D_MODEL ** -0.5
    gmlp_ln_g = 1.0 + 0.1 * jax.random.normal(ks[10], (N_GMLP_LAYERS, BRANCH_WIDTH), f32)
    gmlp_ln_b = 0.02 * jax.random.normal(ks[11], (N_GMLP_LAYERS, BRANCH_WIDTH), f32)
    gmlp_w_s = jax.random.normal(ks[12], (N_GMLP_LAYERS, GMLP_GROUPS, GMLP_BLOCK, GMLP_BLOCK), f32) * GMLP_BLOCK ** -0.5
    gmlp_b_s = 1.0 + 0.1 * jax.random.normal(ks[13], (N_GMLP_LAYERS, GMLP_GROUPS, GMLP_BLOCK), f32)
    return {"x": x, "mem": mem, "pre_norm_g": pre_norm_g, "post_norm_g": post_norm_g,
            "mem_norm_g": mem_norm_g, "w_mem_kv": w_mem_kv, "w_out": w_out,
            "conv_w_in": conv_w_in, "conv_w": conv_w, "gmlp_w_in": gmlp_w_in,
            "gmlp_ln_g": gmlp_ln_g, "gmlp_ln_b": gmlp_ln_b, "gmlp_w_s": gmlp_w_s,
            "gmlp_b_s": gmlp_b_s}


def reference(x, mem, pre_norm_g, post_norm_g, mem_norm_g, w_mem_kv, w_out,
              conv_w_in, conv_w, gmlp_w_in, gmlp_ln_g, gmlp_ln_b, gmlp_w_s, gmlp_b_s):
    for i in range(DEPTH):
        h = rms_norm(x, pre_norm_g[i])
        j = i // N_MIXERS
        if i % N_MIXERS == 0:
            proj = h @ conv_w_in[j]
            paths, q, z = jnp.split(proj, [3 * BRANCH_WIDTH, 3 * BRANCH_WIDTH + MEM_WIDTH], axis=-1)
            branch = conv_branch(paths, conv_w[j])
        else:
            proj = h @ gmlp_w_in[j]
            paths, q, z = jnp.split(proj, [2 * BRANCH_WIDTH, 2 * BRANCH_WIDTH + MEM_WIDTH], axis=-1)
            branch = gmlp_branch(paths, gmlp_ln_g[j], gmlp_ln_b[j], gmlp_w_s[j], gmlp_b_s[j])
        mem_out = memory_attention(q, mem, mem_norm_g[i], w_mem_kv[i])
        y = jnp.concatenate([branch, mem_out], axis=-1) * jax.nn.silu(z)
        y = y @ w_out[i]
        x = x + rms_norm(y, post_norm_g[i])
    return x
```

```python
import numpy as np
from contextlib import ExitStack
import concourse.bass as bass
import concourse.mybir as mybir
from concourse.bass_utils import run_bass_kernel_spmd

F32 = mybir.dt.float32
BF16 = mybir.dt.bfloat16
AF = mybir.ActivationFunctionType
ALU = mybir.AluOpType

PE, ACT, DVE, POOL, SP = "tensor", "scalar", "vector", "gpsimd", "sync"
ENGINES = [PE, ACT, DVE, POOL, SP]


class Tok:
    __slots__ = ("sem", "val")

    def __init__(self, sem, val=None):
        self.sem = sem
        self.val = val


class Prog:
    def __init__(self, nc, stack):
        self.nc = nc
        self.stack = stack
        self.q = {e: [] for e in ENGINES}
        self.esem = {}
        for e in (PE, ACT, DVE, POOL):
            self.esem[e] = stack.enter_context(nc.semaphore("es_" + e))
        self.cnt = {e: 0 for e in (PE, ACT, DVE, POOL)}
        self.pending_pe = []
        self.last_w = {}
        self.readers = {}
        self.nsem = 0

    def new_sem(self, name=None):
        self.nsem += 1
        return self.stack.enter_context(self.nc.semaphore(name or f"s{self.nsem}"))

    def _deps(self, reads, writes):
        deps = []
        for k in reads:
            t = self.last_w.get(k)
            if t is not None:
                deps.append(t)
        for k in writes:
            t = self.last_w.get(k)
            if t is not None:
                deps.append(t)
            deps.extend(self.readers.get(k, ()))
        return deps

    def _commit(self, tok, reads, writes):
        for k in reads:
            self.readers.setdefault(k, []).append(tok)
        for k in writes:
            self.last_w[k] = tok
            self.readers[k] = []

    def op(self, eng, fn, reads=(), writes=(), inc=True):
        deps = self._deps(reads, writes)
        if eng == PE:
            deps = [t for t in deps if t.sem is not self.esem[PE]]
            tok = Tok(self.esem[PE])
            self.pending_pe.append(tok)
            if inc:
                self.cnt[PE] += 1
                for t in self.pending_pe:
                    t.val = self.cnt[PE]
                self.pending_pe = []
        else:
            self.cnt[eng] += 1
            tok = Tok(self.esem[eng], self.cnt[eng])
        self._commit(tok, reads, writes)
        self.q[eng].append((deps, fn, (self.esem[eng], 1) if inc else None, tok))
        return tok

    def dma(self, eng, fn, sem, semval, reads=(), writes=()):
        deps = self._deps(reads, writes)
        tok = Tok(sem, semval)
        self._commit(tok, reads, writes)
        self.q[eng].append((deps, fn, (sem, 16), tok))
        return tok

    def wait_all(self, eng, toks):
        self.q[eng].append((list(toks), None, None, None))

    def emit(self):
        nc = self.nc
        assert not self.pending_pe
        with nc.Block() as block:
            for ename in ENGINES:
                items = self.q[ename]

                def body(eng, items=items):
                    seen = {}
                    for deps, fn, inc, tok in items:
                        need = {}
                        for t in deps:
                            if t is tok:
                                continue
                            sid = id(t.sem)
                            if t.val > seen.get(sid, 0):
                                if sid not in need or need[sid][1] < t.val:
                                    need[sid] = (t.sem, t.val)
                        for sid, (s, v) in need.items():
                            eng.wait_ge(s, v)
                            seen[sid] = v
                        if fn is not None:
                            ins = fn(eng)
                            if inc is not None:
                                ins.then_inc(inc[0], inc[1])

                getattr(block, ename)(body)


class Cfg:
    def __init__(self, D=4096, BR=3072, HEADS=4, GG=8, NMEM=256, DEPTH=4, TOWN=2048,
                 TILE_BLOCKS=2, NCORES=8, R=6):
        self.D, self.BR, self.HEADS, self.GG, self.NMEM, self.DEPTH = D, BR, HEADS, GG, NMEM, DEPTH
        self.TOWN, self.NCORES, self.R = TOWN, NCORES, R
        self.MW = HEADS * 256
        assert D == BR + self.MW
        self.KC = D // 128
        self.BC = BR // 128
        self.MC = self.MW // 128
        self.GCH = self.BC // GG
        assert self.GCH * GG == self.BC
        self.HALO = 128
        self.TLOC = 2 + self.HALO + TOWN
        nblocks = (self.HALO + TOWN) // 128
        self.tiles = []
        b = 0
        while b < nblocks:
            nb = min(TILE_BLOCKS, nblocks - b)
            self.tiles.append((b, nb))
            b += nb
        self.NBMAX = TILE_BLOCKS
        self.TW = 2 + 128 * TILE_BLOCKS
        self.NCONV = (DEPTH + 1) // 2
        self.NGM = DEPTH // 2
        self.CONV_IN = 3 * BR + self.MW + D
        self.GM_IN = 2 * BR + self.MW + D
        o = 0
        self.o_gpre = o; o += DEPTH * self.KC
        self.o_gpost = o; o += DEPTH * self.KC
        self.o_gmem = o; o += DEPTH * self.KC
        self.o_cw = o; o += self.NCONV * 3 * self.BC
        self.o_lng = o; o += max(self.NGM, 1) * self.BC
        self.o_lnb = o; o += max(self.NGM, 1) * self.BC
        self.NCST = o

    def layer_units(self, l):
        BC, MC = self.BC, self.MC
        u = []
        if l % 2 == 0:
            oB, oC, oH, oQ, oZ = 0, BC, 2 * BC, 3 * BC, 3 * BC + MC
            for i in range(BC):
                u += [("C", i, oC + i), ("H", i, oH + i), ("B", i, oB + i), ("Z", i, oZ + i)]
        else:
            oU, oV, oQ, oZ = 0, BC, 2 * BC, 2 * BC + MC
            for i in range(BC):
                u.append(("V", i, oV + i))
            for i in range(BC):
                u += [("U", i, oU + i), ("Z", i, oZ + i)]
        for h in range(self.HEADS):
            u += [("Q", 2 * h, oQ + 2 * h), ("Q", 2 * h + 1, oQ + 2 * h + 1),
                  ("Z", BC + 2 * h, oZ + BC + 2 * h), ("Z", BC + 2 * h + 1, oZ + BC + 2 * h + 1)]
        return u


def _units_from(W, chunks, KC):
    K, NC = W.shape
    Wr = W.reshape(KC, 128, NC // 128, 128).transpose(2, 1, 0, 3)
    return np.ascontiguousarray(Wr[np.asarray(chunks)]).reshape(len(chunks), 128, KC * 128)


def build_program(cfg):
    nc = bass.Bass("TRN2", target_bir_lowering=False)
    D, KC, BC, MC, HEADS, GG, GCH = cfg.D, cfg.KC, cfg.BC, cfg.MC, cfg.HEADS, cfg.GG, cfg.GCH
    L, TW, NM, R, BR = cfg.DEPTH, cfg.TW, cfg.NMEM, cfg.R, cfg.BR
    assert NM == 256 and TW >= NM
    WU = KC * 128
    NBS = max(cfg.NGM, 1) * GG * 128
    xT_d = nc.dram_tensor("xT", [D, cfg.TLOC], F32, kind="ExternalInput").ap()
    memT_d = nc.dram_tensor("memT", [D, NM], F32, kind="ExternalInput").ap()
    cst_d = nc.dram_tensor("cst", [128, cfg.NCST], F32, kind="ExternalInput").ap()
    cst2_d = nc.dram_tensor("cst2", [128, 2 * NBS + 256], F32, kind="ExternalInput").ap()
    w_d = []
    for l in range(L):
        n_u = len(cfg.layer_units(l)) + KC
        w_d.append(nc.dram_tensor(f"w{l}", [n_u, 128, WU], F32, kind="ExternalInput").ap())
    wkv_d = nc.dram_tensor("wkv", [L * 2 * MC, 128, WU], F32, kind="ExternalInput").ap()
    out_d = nc.dram_tensor("outT", [D, cfg.TOWN], F32, kind="ExternalOutput").ap()
    kvs_d = nc.dram_tensor("kvs", [L, 128, MC * NM + 2 * cfg.MW], BF16).ap()

    xT_v = xT_d.rearrange("(c p) t -> p c t", p=128)
    out_v = out_d.rearrange("(c p) t -> p c t", p=128)
    memT_v = memT_d.rearrange("(c p) t -> p c t", p=128)

    with ExitStack() as st:
        P = Prog(nc, st)

        def sb(name, shape, dt):
            return st.enter_context(nc.sbuf_tensor(name, shape, dt))

        xT = sb("xT_sb", [128, KC, TW], F32)
        hT = sb("hT_sb", [128, KC, TW], BF16)
        yT = sb("yT_sb", [128, KC, TW], BF16)
        y2 = sb("y2_sb", [128, KC, TW], F32)
        wring = [sb(f"wr{i}", [128, WU], BF16) for i in range(R)]
        wsem = [P.new_sem(f"wsem{i}") for i in range(R)]
        vln = sb("vln", [128, cfg.NBMAX, BC * 128], BF16)
        qT = sb("qT", [128, 2, TW], BF16)
        pT = sb("pT", [128, 2, TW], BF16)
        KT = sb("KT", [128, MC * NM], BF16)
        Vt = sb("Vt", [128, 2 * cfg.MW], BF16)
        cst = sb("cst_sb", [128, cfg.NCST], F32)
        bsb = sb("bsb", [128, NBS], F32)
        wsTm = sb("wsTm", [128, NBS], BF16)
        stg = sb("stg", [128, 256], F32)
        ones_f = sb("ones_f", [128, 128], F32)
        ones_b = sb("ones_b", [128, 128], BF16)
        cstate = sb("cstate", [128, max(cfg.NCONV, 1) * BC * 2], F32)
        NT = 14
        tmp = [sb(f"tmp{i}", [128, TW], F32) for i in range(NT)]
        ps = [st.enter_context(nc.psum_tensor(f"ps{i}", [128, 512], F32)) for i in range(8)]
        LT = [(sb(f"lt{i}", [128, TW], F32), ("lt", i)) for i in range(4)]
        ident = stg[:, 128:256]
        maskT = stg[:, 0:128]

        sem_misc = P.new_sem("misc")
        misc_cnt = [0]
        XG = 4
        NXG = (KC + XG - 1) // XG
        sem_xg = [P.new_sem(f"xio{g}") for g in range(NXG)]
        xg_cnt = [0] * NXG

        def x_dma(out_fn, in_fn, is_load):
            toks = []
            for g in range(NXG):
                ca, cb = g * XG, min(KC, (g + 1) * XG)
                xg_cnt[g] += 1
                keys = [("x", c) for c in range(ca, cb)]
                o, i_ = out_fn(ca, cb), in_fn(ca, cb)
                toks.append(P.dma(SP, lambda e, o=o, i_=i_: e.dma_start(out=o, in_=i_), sem_xg[g], 16 * xg_cnt[g],
                                  reads=([] if is_load else keys), writes=(keys if is_load else [])))
            return toks
        sem_k = P.new_sem("kio")
        sem_v = P.new_sem("vio")
        k_cnt = [0]
        v_cnt = [0]

        def dma_misc(eng, out, in_, reads=(), writes=()):
            misc_cnt[0] += 1
            sm = P.new_sem(f"misc{misc_cnt[0]}")
            return P.dma(eng, lambda e: e.dma_start(out=out, in_=in_), sm, 16, reads, writes)

        tctr = [0]

        def T():
            i = tctr[0] % NT
            tctr[0] += 1
            return tmp[i], ("tmp", i)

        bctr = [0]

        def bank():
            b = bctr[0] % 6
            bctr[0] += 1
            return b

        wq = []
        wn = [0]
        wcnt = [0] * R

        def w_issue(idx):
            src = wq[idx]
            s = idx % R
            wcnt[s] += 1
            P.dma(POOL, lambda e, s=s, src=src: e.dma_start(out=wring[s][:], in_=src), wsem[s],
                  16 * wcnt[s], writes=[("w", s)])

        def unit_matmuls(rhs_fn, n, reads, bnk=None):
            i = wn[0]
            wn[0] += 1
            s = i % R
            b = bank() if bnk is None else bnk
            for kc in range(KC):
                rhs = rhs_fn(kc)
                P.op(PE, lambda e, s=s, b=b, kc=kc, rhs=rhs: e.matmul(ps[b][:, 0:n], lhsT=wring[s][:, kc * 128:(kc + 1) * 128],
                                                                     rhs=rhs, start=(kc == 0), stop=(kc == KC - 1)),
                     reads=[("w", s)] + reads, writes=[("ps", b)], inc=(kc == KC - 1))
            if i + R < len(wq):
                w_issue(i + R)
            return b

        for l in range(L):
            for u in range(2 * MC):
                wq.append(wkv_d[l * 2 * MC + u])
        for (b0, nb) in cfg.tiles:
            for l in range(L):
                for u in range(len(cfg.layer_units(l)) + KC):
                    wq.append(w_d[l][u])

        dma_misc(SP, cst[:], cst_d[:, :], writes=["cst"])
        dma_misc(SP, bsb[:], cst2_d[:, 0:NBS], writes=["bsb"])
        dma_misc(SP, stg[:], cst2_d[:, 2 * NBS:2 * NBS + 256], writes=["stg"])
        P.op(DVE, lambda e: e.memset(ones_f[:], 1.0), writes=["ones_f"])
        P.op(DVE, lambda e: e.memset(ones_b[:], 1.0), writes=["ones_b"])
        P.op(DVE, lambda e: e.memset(cstate[:], 0.0), writes=[("cstate", j_, i_) for j_ in range(max(cfg.NCONV, 1)) for i_ in range(BC)])
        for gidx in range(cfg.NGM * GG):
            t, kt = T()
            dma_misc(SP, t[:, 0:128], cst2_d[:, NBS + gidx * 128:NBS + (gidx + 1) * 128], writes=[kt])
            P.op(DVE, lambda e, t=t, gidx=gidx: e.tensor_tensor(out=wsTm[:, gidx * 128:(gidx + 1) * 128], in0=t[:, 0:128],
                                                                 in1=maskT, op=ALU.mult),
                 reads=[kt, "stg"], writes=["wsTm"])
        for i in range(min(R, len(wq))):
            w_issue(i)

        def cs(off, idx):
            return cst[:, off + idx:off + idx + 1]

        def rmsnorm_T(src, ksrc, dst, kdst, goff, c0, c1, eps=1e-6):
            n = c1 - c0
            for c in range(KC):
                t, kt = T()
                P.op(ACT, lambda e, t=t, c=c: e.activation(out=t[:, 0:n], in_=src[:, c, c0:c1], func=AF.Square),
                     reads=[(ksrc, c)], writes=[kt])
                P.op(PE, lambda e, t=t, c=c: e.matmul(ps[7][:, 0:n], lhsT=ones_f[:], rhs=t[:, 0:n], start=(c == 0),
                                                     stop=(c == KC - 1)),
                     reads=[kt, "ones_f"], writes=[("ps", 7)], inc=True)
            rs, krs = LT[0]
            P.op(DVE, lambda e: e.tensor_scalar(out=rs[:, 0:n], in0=ps[7][:, 0:n], scalar1=1.0 / D, scalar2=eps,
                                                op0=ALU.mult, op1=ALU.add), reads=[("ps", 7)], writes=[krs])
            P.op(ACT, lambda e: e.activation(out=rs[:, 0:n], in_=rs[:, 0:n], func=AF.Sqrt), reads=[krs], writes=[krs])
            P.op(DVE, lambda e: e.reciprocal(out=rs[:, 0:n], in_=rs[:, 0:n]), reads=[krs], writes=[krs])
            for c in range(KC):
                eng = DVE
                P.op(eng, lambda e, c=c: e.scalar_tensor_tensor(out=dst[:, c, c0:c1], in0=src[:, c, c0:c1],
                                                                 scalar=cs(goff, c), in1=rs[:, 0:n],
                                                                 op0=ALU.mult, op1=ALU.mult),
                     reads=[(ksrc, c), krs, "cst"], writes=[(kdst, c)])

        allh = [("h", kc) for kc in range(KC)]
        ally = [("y", kc) for kc in range(KC)]

        x_dma(lambda ca, cb: xT[:, ca:cb, 0:NM], lambda ca, cb: memT_v[:, ca:cb, :], True)
        for l in range(L):
            rmsnorm_T(xT, "x", hT, "h", cfg.o_gmem + l * KC, 0, NM)
            for u in range(2 * MC):
                b = unit_matmuls(lambda kc: hT[:, kc, 0:NM], NM, allh)
                if u < MC:
                    P.op(ACT, lambda e, b=b, u=u: e.activation(out=KT[:, u * NM:(u + 1) * NM], in_=ps[b][:, 0:NM], func=AF.Copy),
                         reads=[("ps", b)], writes=["KT"])
                else:
                    vc = u - MC
                    t, kt = T()
                    P.op(ACT, lambda e, b=b, t=t: e.activation(out=t[:, 0:NM], in_=ps[b][:, 0:NM], func=AF.Copy),
                         reads=[("ps", b)], writes=[kt])
                    for mc in range(2):
                        b2 = bank()
                        P.op(PE, lambda e, b2=b2, mc=mc, t=t: e.transpose(out=ps[b2][:, 0:128], in_=t[:, mc * 128:(mc + 1) * 128],
                                                                         identity=ident),
                             reads=[kt, "stg"], writes=[("ps", b2)])
                        P.op(DVE, lambda e, b2=b2, mc=mc, vc=vc: e.tensor_copy(
                            out=Vt[:, mc * cfg.MW + vc * 128:mc * cfg.MW + (vc + 1) * 128], in_=ps[b2][:, 0:128]),
                             reads=[("ps", b2)], writes=["Vt"])
            k_cnt[0] += 1
            P.dma(SP, lambda e, l=l: e.dma_start(out=kvs_d[l][:, 0:MC * NM], in_=KT[:]),
                  sem_k, 16 * k_cnt[0], reads=["KT"], writes=[("kvs", l, 0)])
            v_cnt[0] += 1
            P.dma(SP, lambda e, l=l: e.dma_start(out=kvs_d[l][:, MC * NM:], in_=Vt[:]),
                  sem_v, 16 * v_cnt[0], reads=["Vt"], writes=[("kvs", l, 1)])

        def act(out, in_, func, reads, writes, scale=None):
            if scale is None:
                P.op(ACT, lambda e: e.activation(out=out, in_=in_, func=func), reads, writes)
            else:
                P.op(ACT, lambda e: e.activation(out=out, in_=in_, func=func, scale=scale), reads, writes)

        def tt(eng, out, in0, in1, op, reads, writes):
            P.op(eng, lambda e: e.tensor_tensor(out=out, in0=in0, in1=in1, op=op), reads, writes)

        def ts(eng, out, in0, s1, s2, op0, op1, reads, writes):
            if s2 is None:
                P.op(eng, lambda e: e.tensor_scalar(out=out, in0=in0, scalar1=s1, scalar2=None, op0=op0), reads, writes)
            else:
                P.op(eng, lambda e: e.tensor_scalar(out=out, in0=in0, scalar1=s1, scalar2=s2, op0=op0, op1=op1), reads, writes)

        def stt(eng, out, in0, scalar, in1, op0, op1, reads, writes):
            P.op(eng, lambda e: e.scalar_tensor_tensor(out=out, in0=in0, scalar=scalar, in1=in1, op0=op0, op1=op1), reads, writes)

        def cp(eng, out, in_, reads, writes):
            P.op(eng, lambda e: e.tensor_copy(out=out, in_=in_), reads, writes)

        def mm(out, lhsT, rhs, start, stop, reads, writes, inc):
            P.op(PE, lambda e: e.matmul(out, lhsT=lhsT, rhs=rhs, start=start, stop=stop), reads, writes, inc=inc)

        def tr(out, in_, reads, writes):
            P.op(PE, lambda e: e.transpose(out=out, in_=in_, identity=ident), reads, writes)

        GC = 0.7978845608028654

        def gelu_w(b, n, dst_ap, kdst):
            src = ps[b][:, 0:n]
            kb = ("ps", b)
            t1, k1 = T()
            act(t1[:, 0:n], src, AF.Square, [kb], [k1])
            ts(DVE, t1[:, 0:n], t1[:, 0:n], 0.044715, 1.0, ALU.mult, ALU.add, [k1], [k1])
            tt(DVE, t1[:, 0:n], src, t1[:, 0:n], ALU.mult, [k1, kb], [k1])
            act(t1[:, 0:n], t1[:, 0:n], AF.Tanh, [k1], [k1], scale=GC)
            stt(DVE, dst_ap, t1[:, 0:n], 1.0, src, ALU.add, ALU.mult, [k1, kb], [kdst])

        def silu2(b, o0, n, eng=DVE):
            src = ps[b][:, o0:o0 + n]
            kb = ("ps", b)
            t1, k1 = T()
            act(t1[:, 0:n], src, AF.Tanh, [kb], [k1], scale=0.5)
            stt(eng, t1[:, 0:n], t1[:, 0:n], 1.0, src, ALU.add, ALU.mult, [k1, kb], [k1])
            return t1, k1

        def mem_head(h, c0, n2, o0):
            n = n2 + o0
            rhsf = lambda kc: hT[:, kc, c0:c0 + n]
            bq = [unit_matmuls(rhsf, n, allh) for _ in range(2)]
            for dc in range(2):
                act(qT[:, dc, 0:n2], ps[bq[dc]][:, o0:o0 + n2], AF.Copy, [("ps", bq[dc])], [("qT", dc)])
            bz = [unit_matmuls(rhsf, n, allh) for _ in range(2)]
            szm = [silu2(bz[dc], o0, n2) for dc in range(2)]
            for mc in range(2):
                b = bank()
                for dc in range(2):
                    o = (2 * h + dc) * NM + mc * 128
                    mm(ps[b][:, 0:n2], KT[:, o:o + 128], qT[:, dc, 0:n2], dc == 0, dc == 1,
                       ["KT", ("qT", dc)], [("ps", b)], dc == 1)
                act(pT[:, mc, 0:n2], ps[b][:, 0:n2], AF.Exp, [("ps", b)], [("pT", mc)], scale=1.0 / 16.0)
            bd = bank()
            for mc in range(2):
                mm(ps[bd][:, 0:n2], ones_b[:], pT[:, mc, 0:n2], mc == 0, mc == 1, ["ones_b", ("pT", mc)], [("ps", bd)], mc == 1)
            rd, krd = T()
            P.op(DVE, lambda e: e.reciprocal(out=rd[:, 0:n2], in_=ps[bd][:, 0:n2]), [("ps", bd)], [krd])
            for dc in range(2):
                b = bank()
                for mc in range(2):
                    o = mc * cfg.MW + h * 256 + dc * 128
                    mm(ps[b][:, 0:n2], Vt[:, o:o + 128], pT[:, mc, 0:n2], mc == 0, mc == 1, ["Vt", ("pT", mc)], [("ps", b)], mc == 1)
                t, kt = T()
                stt(DVE, t[:, 0:n2], ps[b][:, 0:n2], 0.5, rd[:, 0:n2], ALU.mult, ALU.mult, [("ps", b), krd], [kt])
                sz, ksz = szm[dc]
                ch = BC + 2 * h + dc
                tt(POOL, yT[:, ch, 2:2 + n2], t[:, 0:n2], sz[:, 0:n2], ALU.mult, [kt, ksz], [("y", ch)])

        def conv_group(l, i, c0, twt):
            j = l // 2
            n = twt - c0
            n2 = twt - 2
            o0 = 2 - c0
            cwo = cfg.o_cw + j * 3 * BC
            rhsf = lambda kc: hT[:, kc, c0:twt]
            bC = unit_matmuls(rhsf, n, allh)
            bH = unit_matmuls(rhsf, n, allh)
            bB = unit_matmuls(rhsf, n, allh)
            bZ = unit_matmuls(rhsf, n, allh)
            tc_, ktc = T()
            act(tc_[:, 0:n], ps[bC][:, 0:n], AF.Copy, [("ps", bC)], [ktc])
            g, kg = T()
            so = (j * BC + i) * 2
            kst = ("cstate", j, i)
            if c0 == 2:
                cp(POOL, g[:, 0:2], cstate[:, so:so + 2], [kst], [kg])
            tt(DVE, g[:, c0:twt], ps[bH][:, 0:n], tc_[:, 0:n], ALU.mult, [("ps", bH), ktc], [kg])
            cp(POOL, cstate[:, so:so + 2], g[:, twt - 2:twt], [kg], [kst])
            acc, ka = T()
            ts(DVE, acc[:, 0:n2], g[:, 0:n2], cs(cwo, 0 * BC + i), None, ALU.mult, None, [kg, "cst"], [ka])
            stt(DVE, acc[:, 0:n2], g[:, 1:1 + n2], cs(cwo, 1 * BC + i), acc[:, 0:n2], ALU.mult, ALU.add, [kg, ka, "cst"], [ka])
            stt(DVE, acc[:, 0:n2], g[:, 2:2 + n2], cs(cwo, 2 * BC + i), acc[:, 0:n2], ALU.mult, ALU.add, [kg, ka, "cst"], [ka])
            stt(DVE, acc[:, 0:n2], ps[bB][:, o0:o0 + n2], 0.5, acc[:, 0:n2], ALU.mult, ALU.mult, [("ps", bB), ka], [ka])
            sz, ksz = silu2(bZ, o0, n2)
            tt(POOL, yT[:, i, 2:2 + n2], acc[:, 0:n2], sz[:, 0:n2], ALU.mult, [ka, ksz], [("y", i)])

        def conv_layer(l, c0, twt):
            for i in range(BC):
                conv_group(l, i, c0, twt)
            for h in range(HEADS):
                mem_head(h, c0, twt - 2, 2 - c0)

        def gmlp_vunit(i, twt, first, last):
            n2 = twt - 2
            b = unit_matmuls(lambda kc: hT[:, kc, 2:twt], n2, allh)
            gelu_w(b, n2, y2[:, i, 2:twt], ("y2", i))
            t, kt = T()
            act(t[:, 0:n2], y2[:, i, 2:twt], AF.Square, [("y2", i)], [kt])

            def stats():
                mm(ps[6][:, 0:n2], ones_f[:], y2[:, i, 2:twt], first, last, [("y2", i), "ones_f"], [("ps", 6)], True)
                mm(ps[7][:, 0:n2], ones_f[:], t[:, 0:n2], first, last, [kt, "ones_f"], [("ps", 7)], True)
            return stats

        def gmlp_ln_chunk(j, i, twt, nb, rv, krv, nmr, knmr):
            n2 = twt - 2
            t, kt = T()
            tt(DVE, t[:, 0:n2], y2[:, i, 2:twt], rv[:, 0:n2], ALU.mult, [("y2", i), krv], [kt])
            tt(POOL, t[:, 0:n2], t[:, 0:n2], nmr[:, 0:n2], ALU.subtract, [kt, knmr], [kt])
            ts(DVE, t[:, 0:n2], t[:, 0:n2], cs(cfg.o_lng, j * BC + i), cs(cfg.o_lnb, j * BC + i), ALU.mult, ALU.add, [kt, "cst"], [kt])
            for blk in range(nb):
                b2 = bank()
                tr(ps[b2][:, 0:128], t[:, blk * 128:(blk + 1) * 128], [kt, "stg"], [("ps", b2)])
                act(vln[:, blk, i * 128:(i + 1) * 128], ps[b2][:, 0:128], AF.Copy, [("ps", b2)], [("vln", blk, i)])

        def gmlp_ugroup(j, i, twt, nb):
            n2 = twt - 2
            rhsf = lambda kc: hT[:, kc, 2:twt]
            bU = unit_matmuls(rhsf, n2, allh)
            bZ = unit_matmuls(rhsf, n2, allh)
            ug, kug = T()
            gelu_w(bU, n2, ug[:, 0:n2], kug)
            gi = j * GG + i // GCH
            bF = bank()
            for blk in range(nb):
                mm(ps[bF][:, blk * 128:(blk + 1) * 128], vln[:, blk, i * 128:(i + 1) * 128], wsTm[:, gi * 128:(gi + 1) * 128],
                   True, True, [("vln", blk, i), "wsTm"], [("ps", bF)], blk == nb - 1)
            fb, kfb = T()
            for blk in range(nb):
                tt(DVE, fb[:, blk * 128:(blk + 1) * 128], ps[bF][:, blk * 128:(blk + 1) * 128], bsb[:, gi * 128:(gi + 1) * 128],
                   ALU.add, [("ps", bF), "bsb"], [kfb])
            stt(DVE, fb[:, 0:n2], fb[:, 0:n2], 0.25, ug[:, 0:n2], ALU.mult, ALU.mult, [kfb, kug], [kfb])
            sz, ksz = silu2(bZ, 0, n2)
            tt(POOL, yT[:, i, 2:2 + n2], fb[:, 0:n2], sz[:, 0:n2], ALU.mult, [kfb, ksz], [("y", i)])

        def gmlp_layer(l, c0, twt, nb):
            j = l // 2
            n2 = twt - 2
            assert c0 == 2
            pend = None
            for i in range(BC):
                st_ = gmlp_vunit(i, twt, i == 0, i == BC - 1)
                if pend is not None:
                    pend()
                pend = st_
            pend()
            m, km = LT[1]
            ts(DVE, m[:, 0:n2], ps[6][:, 0:n2], 1.0 / BR, None, ALU.mult, None, [("ps", 6)], [km])
            msq, kmsq = T()
            tt(DVE, msq[:, 0:n2], m[:, 0:n2], m[:, 0:n2], ALU.mult, [km], [kmsq])
            rv, krv = LT[2]
            stt(DVE, rv[:, 0:n2], ps[7][:, 0:n2], 1.0 / BR, msq[:, 0:n2], ALU.mult, ALU.subtract, [("ps", 7), kmsq], [krv])
            ts(DVE, rv[:, 0:n2], rv[:, 0:n2], 4e-5, None, ALU.add, None, [krv], [krv])
            P.op(ACT, lambda e: e.activation(out=rv[:, 0:n2], in_=rv[:, 0:n2], func=AF.Sqrt), [krv], [krv])
            P.op(DVE, lambda e: e.reciprocal(out=rv[:, 0:n2], in_=rv[:, 0:n2]), [krv], [krv])
            nmr, knmr = LT[3]
            tt(DVE, nmr[:, 0:n2], m[:, 0:n2], rv[:, 0:n2], ALU.mult, [km, krv], [knmr])
            for i in range(BC):
                gmlp_ln_chunk(j, i, twt, nb, rv, krv, nmr, knmr)
            for i in range(BC):
                gmlp_ugroup(j, i, twt, nb)
            for h in range(HEADS):
                mem_head(h, 2, n2, 0)

        def out_unit(jo, twt):
            n2 = twt - 2
            b = unit_matmuls(lambda kc: yT[:, kc, 2:twt], n2, ally)
            act(y2[:, jo, 2:twt], ps[b][:, 0:n2], AF.Copy, [("ps", b)], [("y2", jo)])
            t, kt = T()
            act(t[:, 0:n2], ps[b][:, 0:n2], AF.Square, [("ps", b)], [kt])

            def stats():
                mm(ps[7][:, 0:n2], ones_f[:], t[:, 0:n2], jo == 0, jo == KC - 1, [kt, "ones_f"], [("ps", 7)], True)
            return stats

        def resid_chunk(l, c, twt, rs, krs):
            n2 = twt - 2
            t, kt = T()
            stt(DVE, t[:, 0:n2], y2[:, c, 2:twt], cs(cfg.o_gpost, l * KC + c), rs[:, 0:n2], ALU.mult, ALU.mult, [("y2", c), krs, "cst"], [kt])
            tt(POOL, xT[:, c, 2:twt], xT[:, c, 2:twt], t[:, 0:n2], ALU.add, [kt, ("x", c)], [("x", c)])

        def out_proj(l, twt):
            n2 = twt - 2
            pend = None
            for jo in range(KC):
                st_ = out_unit(jo, twt)
                if pend is not None:
                    pend()
                pend = st_
            pend()
            rs, krs = LT[0]
            ts(DVE, rs[:, 0:n2], ps[7][:, 0:n2], 1.0 / D, 1e-6, ALU.mult, ALU.add, [("ps", 7)], [krs])
            P.op(ACT, lambda e: e.activation(out=rs[:, 0:n2], in_=rs[:, 0:n2], func=AF.Sqrt), [krs], [krs])
            P.op(DVE, lambda e: e.reciprocal(out=rs[:, 0:n2], in_=rs[:, 0:n2]), [krs], [krs])
            for c in range(KC):
                resid_chunk(l, c, twt, rs, krs)

        out_toks = []
        for ti, (b0, nb) in enumerate(cfg.tiles):
            twt = 2 + 128 * nb
            t0 = 128 * b0
            x_dma(lambda ca, cb, twt=twt: xT[:, ca:cb, 0:twt], lambda ca, cb, t0=t0, twt=twt: xT_v[:, ca:cb, t0:t0 + twt], True)
            for l in range(L):
                c0 = 0 if (ti == 0 and l == 0) else 2
                k_cnt[0] += 1
                P.dma(SP, lambda e, l=l: e.dma_start(out=KT[:], in_=kvs_d[l][:, 0:MC * NM]), sem_k, 16 * k_cnt[0],
                      reads=[("kvs", l, 0)], writes=["KT"])
                v_cnt[0] += 1
                P.dma(SP, lambda e, l=l: e.dma_start(out=Vt[:], in_=kvs_d[l][:, MC * NM:]), sem_v, 16 * v_cnt[0],
                      reads=[("kvs", l, 1)], writes=["Vt"])
                rmsnorm_T(xT, "x", hT, "h", cfg.o_gpre + l * KC, c0, twt)
                if l % 2 == 0:
                    conv_layer(l, c0, twt)
                else:
                    gmlp_layer(l, c0, twt, nb)
                out_proj(l, twt)
            lo = max(t0 + 2, 2 + cfg.HALO)
            hi = t0 + twt
            if hi > lo:
                out_toks += x_dma(lambda ca, cb, lo=lo, hi=hi: out_v[:, ca:cb, lo - 2 - cfg.HALO:hi - 2 - cfg.HALO],
                                  lambda ca, cb, lo=lo, hi=hi, t0=t0: xT[:, ca:cb, lo - t0:hi - t0], False)
        P.wait_all(SP, out_toks)
        P.emit()
    return nc


def prepare_inputs(cfg, x, mem, pre_norm_g, post_norm_g, mem_norm_g, w_mem_kv, w_out,
                   conv_w_in, conv_w, gmlp_w_in, gmlp_ln_g, gmlp_ln_b, gmlp_w_s, gmlp_b_s):
    D, KC, BC, MC, GG, L = cfg.D, cfg.KC, cfg.BC, cfg.MC, cfg.GG, cfg.DEPTH
    f32 = np.float32
    x = np.asarray(x, f32)[0]
    mem = np.asarray(mem, f32)[0]
    shared = {}
    shared["memT"] = np.ascontiguousarray(mem.T)
    cst = np.zeros((128, cfg.NCST), f32)

    def put(off, vec2d):
        n, W = vec2d.shape
        C = W // 128
        cst[:, off:off + n * C] = vec2d.reshape(n, C, 128).transpose(2, 0, 1).reshape(128, n * C)

    put(cfg.o_gpre, np.asarray(pre_norm_g, f32))
    put(cfg.o_gpost, np.asarray(post_norm_g, f32))
    put(cfg.o_gmem, np.asarray(mem_norm_g, f32))
    cw = np.asarray(conv_w, f32)
    put(cfg.o_cw, cw.reshape(cfg.NCONV * 3, cfg.BR))
    if cfg.NGM:
        put(cfg.o_lng, np.asarray(gmlp_ln_g, f32))
        put(cfg.o_lnb, np.asarray(gmlp_ln_b, f32))
    shared["cst"] = cst
    NBS = max(cfg.NGM, 1) * GG * 128
    cst2 = np.zeros((128, 2 * NBS + 256), f32)
    if cfg.NGM:
        bs = np.asarray(gmlp_b_s, f32).reshape(1, cfg.NGM * GG * 128)
        cst2[:, 0:NBS] = np.broadcast_to(bs, (128, NBS))
        ws = np.asarray(gmlp_w_s, f32)
        cst2[:, NBS:2 * NBS] = ws.transpose(3, 0, 1, 2).reshape(128, NBS)
    s_idx = np.arange(128)[:, None]
    t_idx = np.arange(128)[None, :]
    cst2[:, 2 * NBS:2 * NBS + 128] = (s_idx <= t_idx).astype(f32)
    cst2[:, 2 * NBS + 128:2 * NBS + 256] = np.eye(128, dtype=f32)
    shared["cst2"] = cst2
    for l in range(L):
        units = cfg.layer_units(l)
        chunks = [u[2] for u in units]
        Win = np.asarray(conv_w_in[l // 2] if l % 2 == 0 else gmlp_w_in[l // 2], f32)
        a = _units_from(Win, chunks, KC)
        b = _units_from(np.asarray(w_out[l], f32), list(range(KC)), KC)
        shared[f"w{l}"] = np.concatenate([a, b], axis=0)
    shared["wkv"] = np.concatenate([_units_from(np.asarray(w_mem_kv[l], f32), list(range(2 * MC)), KC) for l in range(L)], axis=0)
    in_maps = []
    pre = 2 + cfg.HALO
    for c in range(cfg.NCORES):
        s = c * cfg.TOWN
        xl = np.zeros((cfg.TLOC, D), f32)
        lo = s - pre
        if lo >= 0:
            xl[:] = x[lo:s + cfg.TOWN]
        else:
            xl[-lo:] = x[0:s + cfg.TOWN]
        m = dict(shared)
        m["xT"] = np.ascontiguousarray(xl.T)
        in_maps.append(m)
    return in_maps


_CACHE = {}


def kernel(x, mem, pre_norm_g, post_norm_g, mem_norm_g, w_mem_kv, w_out,
           conv_w_in, conv_w, gmlp_w_in, gmlp_ln_g, gmlp_ln_b, gmlp_w_s, gmlp_b_s):
    cfg = Cfg()
    in_maps = prepare_inputs(cfg, x, mem, pre_norm_g, post_norm_g, mem_norm_g, w_mem_kv, w_out,
                             conv_w_in, conv_w, gmlp_w_in, gmlp_ln_g, gmlp_ln_b, gmlp_w_s, gmlp_b_s)
    if "nc" not in _CACHE:
        _CACHE["nc"] = build_program(cfg)
    res = run_bass_kernel_spmd(_CACHE["nc"], in_maps, core_ids=list(range(cfg.NCORES)))
    outs = [np.asarray(r["outT"]).T for r in res.results]
    return np.ascontiguousarray(np.concatenate(outs, axis=0)[None]).astype(np.float32)
```

```python
import numpy as np
from contextlib import ExitStack
import concourse.bass as bass
import concourse.mybir as mybir
from concourse.bass_utils import run_bass_kernel_spmd

F32 = mybir.dt.float32
BF16 = mybir.dt.bfloat16
AF = mybir.ActivationFunctionType
ALU = mybir.AluOpType

PE, ACT, DVE, POOL, SP = "tensor", "scalar", "vector", "gpsimd", "sync"
ENGINES = [PE, ACT, DVE, POOL, SP]


class Tok:
    __slots__ = ("sem", "val")

    def __init__(self, sem, val=None):
        self.sem = sem
        self.val = val


class Prog:
    def __init__(self, nc, stack):
        self.nc = nc
        self.stack = stack
        self.q = {e: [] for e in ENGINES}
        self.esem = {}
        for e in (PE, ACT, DVE, POOL):
            self.esem[e] = stack.enter_context(nc.semaphore("es_" + e))
        self.cnt = {e: 0 for e in (PE, ACT, DVE, POOL)}
        self.pending_pe = []
        self.last_w = {}
        self.readers = {}
        self.nsem = 0

    def new_sem(self, name=None):
        self.nsem += 1
        return self.stack.enter_context(self.nc.semaphore(name or f"s{self.nsem}"))

    def _deps(self, reads, writes):
        deps = []
        for k in reads:
            t = self.last_w.get(k)
            if t is not None:
                deps.append(t)
        for k in writes:
            t = self.last_w.get(k)
            if t is not None:
                deps.append(t)
            deps.extend(self.readers.get(k, ()))
        return deps

    def _commit(self, tok, reads, writes):
        for k in reads:
            self.readers.setdefault(k, []).append(tok)
        for k in writes:
            self.last_w[k] = tok
            self.readers[k] = []

    def op(self, eng, fn, reads=(), writes=(), inc=True):
        deps = self._deps(reads, writes)
        if eng == PE:
            deps = [t for t in deps if t.sem is not self.esem[PE]]
            tok = Tok(self.esem[PE])
            self.pending_pe.append(tok)
            if inc:
                self.cnt[PE] += 1
                for t in self.pending_pe:
                    t.val = self.cnt[PE]
                self.pending_pe = []
        else:
            self.cnt[eng] += 1
            tok = Tok(self.esem[eng], self.cnt[eng])
        self._commit(tok, reads, writes)
        self.q[eng].append((deps, fn, (self.esem[eng], 1) if inc else None, tok))
        return tok

    def dma(self, eng, fn, sem, semval, reads=(), writes=()):
        deps = self._deps(reads, writes)
        tok = Tok(sem, semval)
        self._commit(tok, reads, writes)
        self.q[eng].append((deps, fn, (sem, 16), tok))
        return tok

    def wait_all(self, eng, toks):
        self.q[eng].append((list(toks), None, None, None))

    def emit(self):
        nc = self.nc
        assert not self.pending_pe
        with nc.Block() as block:
            for ename in ENGINES:
                items = self.q[ename]

                def body(eng, items=items):
                    seen = {}
                    for deps, fn, inc, tok in items:
                        need = {}
                        for t in deps:
                            if t is tok:
                                continue
                            sid = id(t.sem)
                            if t.val > seen.get(sid, 0):
                                if sid not in need or need[sid][1] < t.val:
                                    need[sid] = (t.sem, t.val)
                        for sid, (s, v) in need.items():
                            eng.wait_ge(s, v)
                            seen[sid] = v
                        if fn is not None:
                            ins = fn(eng)
                            if inc is not None:
                                ins.then_inc(inc[0], inc[1])

                getattr(block, ename)(body)


class Cfg:
    def __init__(self, D=4096, BR=3072, HEADS=4, GG=8, NMEM=256, DEPTH=4, TOWN=2048,
                 TILE_BLOCKS=2, NCORES=8, R=6):
        self.D, self.BR, self.HEADS, self.GG, self.NMEM, self.DEPTH = D, BR, HEADS, GG, NMEM, DEPTH
        self.TOWN, self.NCORES, self.R = TOWN, NCORES, R
        self.MW = HEADS * 256
        assert D == BR + self.MW
        self.KC = D // 128
        self.BC = BR // 128
        self.MC = self.MW // 128
        self.GCH = self.BC // GG
        assert self.GCH * GG == self.BC
        self.HALO = 128
        self.TLOC = 2 + self.HALO + TOWN
        nblocks = (self.HALO + TOWN) // 128
        self.tiles = []
        b = 0
        while b < nblocks:
            nb = min(TILE_BLOCKS, nblocks - b)
            self.tiles.append((b, nb))
            b += nb
        self.NBMAX = TILE_BLOCKS
        self.TW = 2 + 128 * TILE_BLOCKS
        self.NCONV = (DEPTH + 1) // 2
        self.NGM = DEPTH // 2
        self.CONV_IN = 3 * BR + self.MW + D
        self.GM_IN = 2 * BR + self.MW + D
        o = 0
        self.o_gpre = o; o += DEPTH * self.KC
        self.o_gpost = o; o += DEPTH * self.KC
        self.o_gmem = o; o += DEPTH * self.KC
        self.o_cw = o; o += self.NCONV * 3 * self.BC
        self.o_lng = o; o += max(self.NGM, 1) * self.BC
        self.o_lnb = o; o += max(self.NGM, 1) * self.BC
        self.NCST = o

    def layer_units(self, l):
        BC, MC = self.BC, self.MC
        u = []
        if l % 2 == 0:
            oB, oC, oH, oQ, oZ = 0, BC, 2 * BC, 3 * BC, 3 * BC + MC
            for i in range(BC):
                u += [("C", i, oC + i), ("H", i, oH + i), ("B", i, oB + i), ("Z", i, oZ + i)]
        else:
            oU, oV, oQ, oZ = 0, BC, 2 * BC, 2 * BC + MC
            for i in range(BC):
                u.append(("V", i, oV + i))
            for i in range(BC):
                u += [("U", i, oU + i), ("Z", i, oZ + i)]
        for h in range(self.HEADS):
            u += [("Q", 2 * h, oQ + 2 * h), ("Q", 2 * h + 1, oQ + 2 * h + 1),
                  ("Z", BC + 2 * h, oZ + BC + 2 * h), ("Z", BC + 2 * h + 1, oZ + BC + 2 * h + 1)]
        return u


def _units_from(W, chunks, KC):
    K, NC = W.shape
    Wr = W.reshape(KC, 128, NC // 128, 128).transpose(2, 1, 0, 3)
    return np.ascontiguousarray(Wr[np.asarray(chunks)]).reshape(len(chunks), 128, KC * 128)


def build_program(cfg):
    nc = bass.Bass("TRN2", target_bir_lowering=False)
    D, KC, BC, MC, HEADS, GG, GCH = cfg.D, cfg.KC, cfg.BC, cfg.MC, cfg.HEADS, cfg.GG, cfg.GCH
    L, TW, NM, R, BR = cfg.DEPTH, cfg.TW, cfg.NMEM, cfg.R, cfg.BR
    assert NM == 256 and TW >= NM
    WU = KC * 128
    NBS = max(cfg.NGM, 1) * GG * 128
    xT_d = nc.dram_tensor("xT", [D, cfg.TLOC], F32, kind="ExternalInput").ap()
    memT_d = nc.dram_tensor("memT", [D, NM], F32, kind="ExternalInput").ap()
    cst_d = nc.dram_tensor("cst", [128, cfg.NCST], F32, kind="ExternalInput").ap()
    cst2_d = nc.dram_tensor("cst2", [128, 2 * NBS + 256], F32, kind="ExternalInput").ap()
    w_d = []
    for l in range(L):
        n_u = len(cfg.layer_units(l)) + KC
        w_d.append(nc.dram_tensor(f"w{l}", [n_u, 128, WU], F32, kind="ExternalInput").ap())
    wkv_d = nc.dram_tensor("wkv", [L * 2 * MC, 128, WU], F32, kind="ExternalInput").ap()
    out_d = nc.dram_tensor("outT", [D, cfg.TOWN], F32, kind="ExternalOutput").ap()
    kvs_d = nc.dram_tensor("kvs", [L, 128, MC * NM + 2 * cfg.MW], BF16).ap()
    wb_d = [nc.dram_tensor(f"wb{l}", [len(cfg.layer_units(l)) + KC, 128, WU], BF16).ap() for l in range(L)]
    NKV = L * 2 * cfg.MC

    xT_v = xT_d.rearrange("(c p) t -> p c t", p=128)
    out_v = out_d.rearrange("(c p) t -> p c t", p=128)
    memT_v = memT_d.rearrange("(c p) t -> p c t", p=128)

    with ExitStack() as st:
        P = Prog(nc, st)

        def sb(name, shape, dt):
            return st.enter_context(nc.sbuf_tensor(name, shape, dt))

        xT = sb("xT_sb", [128, KC, TW], F32)
        hT = sb("hT_sb", [128, KC, TW], BF16)
        yT = sb("yT_sb", [128, KC, TW], BF16)
        y2 = sb("y2_sb", [128, KC, TW], F32)
        wring = [sb(f"wr{i}", [128, WU], BF16) for i in range(R)]
        wsem = [P.new_sem(f"wsem{i}") for i in range(R)]
        vln = sb("vln", [128, cfg.NBMAX, BC * 128], BF16)
        qT = sb("qT", [128, 2, TW], BF16)
        pT = sb("pT", [128, 2, TW], BF16)
        KT = sb("KT", [128, MC * NM], BF16)
        Vt = sb("Vt", [128, 2 * cfg.MW], BF16)
        cst = sb("cst_sb", [128, cfg.NCST], F32)
        bsb = sb("bsb", [128, NBS], F32)
        wsTm = sb("wsTm", [128, NBS], BF16)
        stg = sb("stg", [128, 256], F32)
        ones_f = sb("ones_f", [128, 128], F32)
        ones_b = sb("ones_b", [128, 128], BF16)
        cstate = sb("cstate", [128, max(cfg.NCONV, 1) * BC * 2], F32)
        NT = 14
        tmp = [sb(f"tmp{i}", [128, TW], F32) for i in range(NT)]
        ps = [st.enter_context(nc.psum_tensor(f"ps{i}", [128, 512], F32)) for i in range(8)]
        LT = [(sb(f"lt{i}", [128, TW], F32), ("lt", i)) for i in range(4)]
        ident = stg[:, 128:256]
        maskT = stg[:, 0:128]

        sem_misc = P.new_sem("misc")
        misc_cnt = [0]
        XG = 4
        NXG = (KC + XG - 1) // XG
        sem_xg = [P.new_sem(f"xio{g}") for g in range(NXG)]
        xg_cnt = [0] * NXG

        def x_dma(out_fn, in_fn, is_load):
            toks = []
            for g in range(NXG):
                ca, cb = g * XG, min(KC, (g + 1) * XG)
                xg_cnt[g] += 1
                keys = [("x", c) for c in range(ca, cb)]
                o, i_ = out_fn(ca, cb), in_fn(ca, cb)
                toks.append(P.dma(SP, lambda e, o=o, i_=i_: e.dma_start(out=o, in_=i_), sem_xg[g], 16 * xg_cnt[g],
                                  reads=([] if is_load else keys), writes=(keys if is_load else [])))
            return toks
        sem_k = P.new_sem("kio")
        sem_v = P.new_sem("vio")
        k_cnt = [0]
        v_cnt = [0]

        def dma_misc(eng, out, in_, reads=(), writes=()):
            misc_cnt[0] += 1
            sm = P.new_sem(f"misc{misc_cnt[0]}")
            return P.dma(eng, lambda e: e.dma_start(out=out, in_=in_), sm, 16, reads, writes)

        tctr = [0]

        def T():
            i = tctr[0] % NT
            tctr[0] += 1
            return tmp[i], ("tmp", i)

        bctr = [0]

        def bank():
            b = bctr[0] % 6
            bctr[0] += 1
            return b

        wq = []
        wn = [0]
        wcnt = [0] * R

        wbsem = [P.new_sem(f"wbsem{i}") for i in range(R)]
        wbcnt = [0] * R
        wsem2 = [P.new_sem(f"wsemh{i}") for i in range(R)]
        wcnt2 = [0] * R

        def w_issue(idx):
            src, scr, mode = wq[idx]
            s = idx % R
            wcnt[s] += 1
            if mode == 2:
                wcnt[s] -= 1
                wcnt2[s] += 1
                P.dma(SP, lambda e, s=s, scr=scr: e.dma_start(out=wring[s][:], in_=scr), wsem2[s],
                      16 * wcnt2[s], reads=[("wb", (idx - NKV) % UPT)], writes=[("w", s)])
                return
            P.dma(POOL, lambda e, s=s, src=src: e.dma_start(out=wring[s][:], in_=src), wsem[s],
                  16 * wcnt[s], writes=[("w", s)])
            if mode == 1:
                wbcnt[s] += 1
                P.dma(SP, lambda e, s=s, scr=scr: e.dma_start(out=scr, in_=wring[s][:]), wbsem[s],
                      16 * wbcnt[s], reads=[("w", s)], writes=[("wb", (idx - NKV) % UPT)])

        def unit_matmuls(rhs_fn, n, reads, bnk=None):
            i = wn[0]
            wn[0] += 1
            s = i % R
            b = bank() if bnk is None else bnk
            for kc in range(KC):
                rhs = rhs_fn(kc)
                P.op(PE, lambda e, s=s, b=b, kc=kc, rhs=rhs: e.matmul(ps[b][:, 0:n], lhsT=wring[s][:, kc * 128:(kc + 1) * 128],
                                                                     rhs=rhs, start=(kc == 0), stop=(kc == KC - 1)),
                     reads=[("w", s)] + reads, writes=[("ps", b)], inc=(kc == KC - 1))
            if i + R < len(wq):
                w_issue(i + R)
            return b

        NKV = L * 2 * MC
        UPT = sum(len(cfg.layer_units(l)) + KC for l in range(L))
        for l in range(L):
            for u in range(2 * MC):
                wq.append((wkv_d[l * 2 * MC + u], None, 0))
        assert NKV < UPT
        KVPAD = NKV
        multi = len(cfg.tiles) > 1
        for ti_, (b0, nb) in enumerate(cfg.tiles):
            for l in range(L):
                for u in range(len(cfg.layer_units(l)) + KC):
                    if not multi:
                        wq.append((w_d[l][u], None, 0))
                    elif ti_ == 0:
                        wq.append((w_d[l][u], wb_d[l][u], 1))
                    else:
                        wq.append((None, wb_d[l][u], 2))

        dma_misc(SP, cst[:], cst_d[:, :], writes=["cst"])
        dma_misc(SP, bsb[:], cst2_d[:, 0:NBS], writes=["bsb"])
        dma_misc(SP, stg[:], cst2_d[:, 2 * NBS:2 * NBS + 256], writes=["stg"])
        P.op(DVE, lambda e: e.memset(ones_f[:], 1.0), writes=["ones_f"])
        P.op(DVE, lambda e: e.memset(ones_b[:], 1.0), writes=["ones_b"])
        P.op(DVE, lambda e: e.memset(cstate[:], 0.0), writes=[("cstate", j_, i_) for j_ in range(max(cfg.NCONV, 1)) for i_ in range(BC)])
        for gidx in range(cfg.NGM * GG):
            t, kt = T()
            dma_misc(SP, t[:, 0:128], cst2_d[:, NBS + gidx * 128:NBS + (gidx + 1) * 128], writes=[kt])
            P.op(DVE, lambda e, t=t, gidx=gidx: e.tensor_tensor(out=wsTm[:, gidx * 128:(gidx + 1) * 128], in0=t[:, 0:128],
                                                                 in1=maskT, op=ALU.mult),
                 reads=[kt, "stg"], writes=["wsTm"])
        for i in range(min(R, len(wq))):
            w_issue(i)

        def cs(off, idx):
            return cst[:, off + idx:off + idx + 1]

        def rmsnorm_T(src, ksrc, dst, kdst, goff, c0, c1, eps=1e-6):
            n = c1 - c0
            for c in range(KC):
                t, kt = T()
                P.op(ACT, lambda e, t=t, c=c: e.activation(out=t[:, 0:n], in_=src[:, c, c0:c1], func=AF.Square),
                     reads=[(ksrc, c)], writes=[kt])
                P.op(PE, lambda e, t=t, c=c: e.matmul(ps[7][:, 0:n], lhsT=ones_f[:], rhs=t[:, 0:n], start=(c == 0),
                                                     stop=(c == KC - 1)),
                     reads=[kt, "ones_f"], writes=[("ps", 7)], inc=True)
            rs, krs = LT[0]
            P.op(DVE, lambda e: e.tensor_scalar(out=rs[:, 0:n], in0=ps[7][:, 0:n], scalar1=1.0 / D, scalar2=eps,
                                                op0=ALU.mult, op1=ALU.add), reads=[("ps", 7)], writes=[krs])
            P.op(ACT, lambda e: e.activation(out=rs[:, 0:n], in_=rs[:, 0:n], func=AF.Sqrt), reads=[krs], writes=[krs])
            P.op(DVE, lambda e: e.reciprocal(out=rs[:, 0:n], in_=rs[:, 0:n]), reads=[krs], writes=[krs])
            for c in range(KC):
                eng = DVE
                P.op(eng, lambda e, c=c: e.scalar_tensor_tensor(out=dst[:, c, c0:c1], in0=src[:, c, c0:c1],
                                                                 scalar=cs(goff, c), in1=rs[:, 0:n],
                                                                 op0=ALU.mult, op1=ALU.mult),
                     reads=[(ksrc, c), krs, "cst"], writes=[(kdst, c)])

        allh = [("h", kc) for kc in range(KC)]
        ally = [("y", kc) for kc in range(KC)]

        x_dma(lambda ca, cb: xT[:, ca:cb, 0:NM], lambda ca, cb: memT_v[:, ca:cb, :], True)
        for l in range(L):
            rmsnorm_T(xT, "x", hT, "h", cfg.o_gmem + l * KC, 0, NM)
            for u in range(2 * MC):
                b = unit_matmuls(lambda kc: hT[:, kc, 0:NM], NM, allh)
                if u < MC:
                    P.op(ACT, lambda e, b=b, u=u: e.activation(out=KT[:, u * NM:(u + 1) * NM], in_=ps[b][:, 0:NM], func=AF.Copy),
                         reads=[("ps", b)], writes=["KT"])
                else:
                    vc = u - MC
                    t, kt = T()
                    P.op(ACT, lambda e, b=b, t=t: e.activation(out=t[:, 0:NM], in_=ps[b][:, 0:NM], func=AF.Copy),
                         reads=[("ps", b)], writes=[kt])
                    for mc in range(2):
                        b2 = bank()
                        P.op(PE, lambda e, b2=b2, mc=mc, t=t: e.transpose(out=ps[b2][:, 0:128], in_=t[:, mc * 128:(mc + 1) * 128],
                                                                         identity=ident),
                             reads=[kt, "stg"], writes=[("ps", b2)])
                        P.op(DVE, lambda e, b2=b2, mc=mc, vc=vc: e.tensor_copy(
                            out=Vt[:, mc * cfg.MW + vc * 128:mc * cfg.MW + (vc + 1) * 128], in_=ps[b2][:, 0:128]),
                             reads=[("ps", b2)], writes=["Vt"])
            k_cnt[0] += 1
            P.dma(SP, lambda e, l=l: e.dma_start(out=kvs_d[l][:, 0:MC * NM], in_=KT[:]),
                  sem_k, 16 * k_cnt[0], reads=["KT"], writes=[("kvs", l, 0)])
            v_cnt[0] += 1
            P.dma(SP, lambda e, l=l: e.dma_start(out=kvs_d[l][:, MC * NM:], in_=Vt[:]),
                  sem_v, 16 * v_cnt[0], reads=["Vt"], writes=[("kvs", l, 1)])

        def act(out, in_, func, reads, writes, scale=None):
            if scale is None:
                P.op(ACT, lambda e: e.activation(out=out, in_=in_, func=func), reads, writes)
            else:
                P.op(ACT, lambda e: e.activation(out=out, in_=in_, func=func, scale=scale), reads, writes)

        def tt(eng, out, in0, in1, op, reads, writes):
            P.op(eng, lambda e: e.tensor_tensor(out=out, in0=in0, in1=in1, op=op), reads, writes)

        def ts(eng, out, in0, s1, s2, op0, op1, reads, writes):
            if s2 is None:
                P.op(eng, lambda e: e.tensor_scalar(out=out, in0=in0, scalar1=s1, scalar2=None, op0=op0), reads, writes)
            else:
                P.op(eng, lambda e: e.tensor_scalar(out=out, in0=in0, scalar1=s1, scalar2=s2, op0=op0, op1=op1), reads, writes)

        def stt(eng, out, in0, scalar, in1, op0, op1, reads, writes):
            P.op(eng, lambda e: e.scalar_tensor_tensor(out=out, in0=in0, scalar=scalar, in1=in1, op0=op0, op1=op1), reads, writes)

        def cp(eng, out, in_, reads, writes):
            P.op(eng, lambda e: e.tensor_copy(out=out, in_=in_), reads, writes)

        def mm(out, lhsT, rhs, start, stop, reads, writes, inc):
            P.op(PE, lambda e: e.matmul(out, lhsT=lhsT, rhs=rhs, start=start, stop=stop), reads, writes, inc=inc)

        def tr(out, in_, reads, writes):
            P.op(PE, lambda e: e.transpose(out=out, in_=in_, identity=ident), reads, writes)

        GC = 0.7978845608028654

        def gelu_w(b, n, dst_ap, kdst):
            src = ps[b][:, 0:n]
            kb = ("ps", b)
            t1, k1 = T()
            act(t1[:, 0:n], src, AF.Square, [kb], [k1])
            ts(DVE, t1[:, 0:n], t1[:, 0:n], 0.044715, 1.0, ALU.mult, ALU.add, [k1], [k1])
            tt(DVE, t1[:, 0:n], src, t1[:, 0:n], ALU.mult, [k1, kb], [k1])
            act(t1[:, 0:n], t1[:, 0:n], AF.Tanh, [k1], [k1], scale=GC)
            stt(DVE, dst_ap, t1[:, 0:n], 1.0, src, ALU.add, ALU.mult, [k1, kb], [kdst])

        def silu2(b, o0, n, eng=DVE):
            src = ps[b][:, o0:o0 + n]
            kb = ("ps", b)
            t1, k1 = T()
            act(t1[:, 0:n], src, AF.Tanh, [kb], [k1], scale=0.5)
            stt(eng, t1[:, 0:n], t1[:, 0:n], 1.0, src, ALU.add, ALU.mult, [k1, kb], [k1])
            return t1, k1

        def mem_head(h, c0, n2, o0):
            n = n2 + o0
            rhsf = lambda kc: hT[:, kc, c0:c0 + n]
            bq = [unit_matmuls(rhsf, n, allh) for _ in range(2)]
            for dc in range(2):
                act(qT[:, dc, 0:n2], ps[bq[dc]][:, o0:o0 + n2], AF.Copy, [("ps", bq[dc])], [("qT", dc)])
            bz = [unit_matmuls(rhsf, n, allh) for _ in range(2)]
            szm = [silu2(bz[dc], o0, n2) for dc in range(2)]
            for mc in range(2):
                b = bank()
                for dc in range(2):
                    o = (2 * h + dc) * NM + mc * 128
                    mm(ps[b][:, 0:n2], KT[:, o:o + 128], qT[:, dc, 0:n2], dc == 0, dc == 1,
                       ["KT", ("qT", dc)], [("ps", b)], dc == 1)
                act(pT[:, mc, 0:n2], ps[b][:, 0:n2], AF.Exp, [("ps", b)], [("pT", mc)], scale=1.0 / 16.0)
            bd = bank()
            for mc in range(2):
                mm(ps[bd][:, 0:n2], ones_b[:], pT[:, mc, 0:n2], mc == 0, mc == 1, ["ones_b", ("pT", mc)], [("ps", bd)], mc == 1)
            rd, krd = T()
            P.op(DVE, lambda e: e.reciprocal(out=rd[:, 0:n2], in_=ps[bd][:, 0:n2]), [("ps", bd)], [krd])
            for dc in range(2):
                b = bank()
                for mc in range(2):
                    o = mc * cfg.MW + h * 256 + dc * 128
                    mm(ps[b][:, 0:n2], Vt[:, o:o + 128], pT[:, mc, 0:n2], mc == 0, mc == 1, ["Vt", ("pT", mc)], [("ps", b)], mc == 1)
                t, kt = T()
                stt(DVE, t[:, 0:n2], ps[b][:, 0:n2], 0.5, rd[:, 0:n2], ALU.mult, ALU.mult, [("ps", b), krd], [kt])
                sz, ksz = szm[dc]
                ch = BC + 2 * h + dc
                tt(POOL, yT[:, ch, 2:2 + n2], t[:, 0:n2], sz[:, 0:n2], ALU.mult, [kt, ksz], [("y", ch)])

        def conv_group(l, i, c0, twt):
            j = l // 2
            n = twt - c0
            n2 = twt - 2
            o0 = 2 - c0
            cwo = cfg.o_cw + j * 3 * BC
            rhsf = lambda kc: hT[:, kc, c0:twt]
            bC = unit_matmuls(rhsf, n, allh)
            bH = unit_matmuls(rhsf, n, allh)
            bB = unit_matmuls(rhsf, n, allh)
            bZ = unit_matmuls(rhsf, n, allh)
            tc_, ktc = T()
            act(tc_[:, 0:n], ps[bC][:, 0:n], AF.Copy, [("ps", bC)], [ktc])
            g, kg = T()
            so = (j * BC + i) * 2
            kst = ("cstate", j, i)
            if c0 == 2:
                cp(POOL, g[:, 0:2], cstate[:, so:so + 2], [kst], [kg])
            tt(DVE, g[:, c0:twt], ps[bH][:, 0:n], tc_[:, 0:n], ALU.mult, [("ps", bH), ktc], [kg])
            cp(POOL, cstate[:, so:so + 2], g[:, twt - 2:twt], [kg], [kst])
            acc, ka = T()
            ts(DVE, acc[:, 0:n2], g[:, 0:n2], cs(cwo, 0 * BC + i), None, ALU.mult, None, [kg, "cst"], [ka])
            stt(DVE, acc[:, 0:n2], g[:, 1:1 + n2], cs(cwo, 1 * BC + i), acc[:, 0:n2], ALU.mult, ALU.add, [kg, ka, "cst"], [ka])
            stt(DVE, acc[:, 0:n2], g[:, 2:2 + n2], cs(cwo, 2 * BC + i), acc[:, 0:n2], ALU.mult, ALU.add, [kg, ka, "cst"], [ka])
            stt(DVE, acc[:, 0:n2], ps[bB][:, o0:o0 + n2], 0.5, acc[:, 0:n2], ALU.mult, ALU.mult, [("ps", bB), ka], [ka])
            sz, ksz = silu2(bZ, o0, n2)
            tt(POOL, yT[:, i, 2:2 + n2], acc[:, 0:n2], sz[:, 0:n2], ALU.mult, [ka, ksz], [("y", i)])

        def conv_layer(l, c0, twt):
            for i in range(BC):
                conv_group(l, i, c0, twt)
            for h in range(HEADS):
                mem_head(h, c0, twt - 2, 2 - c0)

        def gmlp_vunit(i, twt, first, last):
            n2 = twt - 2
            b = unit_matmuls(lambda kc: hT[:, kc, 2:twt], n2, allh)
            gelu_w(b, n2, y2[:, i, 2:twt], ("y2", i))
            t, kt = T()
            act(t[:, 0:n2], y2[:, i, 2:twt], AF.Square, [("y2", i)], [kt])

            def stats():
                mm(ps[6][:, 0:n2], ones_f[:], y2[:, i, 2:twt], first, last, [("y2", i), "ones_f"], [("ps", 6)], True)
                mm(ps[7][:, 0:n2], ones_f[:], t[:, 0:n2], first, last, [kt, "ones_f"], [("ps", 7)], True)
            return stats

        def gmlp_ln_chunk(j, i, twt, nb, rv, krv, nmr, knmr):
            n2 = twt - 2
            t, kt = T()
            tt(DVE, t[:, 0:n2], y2[:, i, 2:twt], rv[:, 0:n2], ALU.mult, [("y2", i), krv], [kt])
            tt(POOL, t[:, 0:n2], t[:, 0:n2], nmr[:, 0:n2], ALU.subtract, [kt, knmr], [kt])
            ts(DVE, t[:, 0:n2], t[:, 0:n2], cs(cfg.o_lng, j * BC + i), cs(cfg.o_lnb, j * BC + i), ALU.mult, ALU.add, [kt, "cst"], [kt])
            for blk in range(nb):
                b2 = bank()
                tr(ps[b2][:, 0:128], t[:, blk * 128:(blk + 1) * 128], [kt, "stg"], [("ps", b2)])
                act(vln[:, blk, i * 128:(i + 1) * 128], ps[b2][:, 0:128], AF.Copy, [("ps", b2)], [("vln", blk, i)])

        def gmlp_ugroup(j, i, twt, nb):
            n2 = twt - 2
            rhsf = lambda kc: hT[:, kc, 2:twt]
            bU = unit_matmuls(rhsf, n2, allh)
            bZ = unit_matmuls(rhsf, n2, allh)
            ug, kug = T()
            gelu_w(bU, n2, ug[:, 0:n2], kug)
            gi = j * GG + i // GCH
            bF = bank()
            for blk in range(nb):
                mm(ps[bF][:, blk * 128:(blk + 1) * 128], vln[:, blk, i * 128:(i + 1) * 128], wsTm[:, gi * 128:(gi + 1) * 128],
                   True, True, [("vln", blk, i), "wsTm"], [("ps", bF)], blk == nb - 1)
            fb, kfb = T()
            for blk in range(nb):
                tt(DVE, fb[:, blk * 128:(blk + 1) * 128], ps[bF][:, blk * 128:(blk + 1) * 128], bsb[:, gi * 128:(gi + 1) * 128],
                   ALU.add, [("ps", bF), "bsb"], [kfb])
            stt(DVE, fb[:, 0:n2], fb[:, 0:n2], 0.25, ug[:, 0:n2], ALU.mult, ALU.mult, [kfb, kug], [kfb])
            sz, ksz = silu2(bZ, 0, n2)
            tt(POOL, yT[:, i, 2:2 + n2], fb[:, 0:n2], sz[:, 0:n2], ALU.mult, [kfb, ksz], [("y", i)])

        def gmlp_layer(l, c0, twt, nb):
            j = l // 2
            n2 = twt - 2
            assert c0 == 2
            pend = None
            for i in range(BC):
                st_ = gmlp_vunit(i, twt, i == 0, i == BC - 1)
                if pend is not None:
                    pend()
                pend = st_
            pend()
            m, km = LT[1]
            ts(DVE, m[:, 0:n2], ps[6][:, 0:n2], 1.0 / BR, None, ALU.mult, None, [("ps", 6)], [km])
            msq, kmsq = T()
            tt(DVE, msq[:, 0:n2], m[:, 0:n2], m[:, 0:n2], ALU.mult, [km], [kmsq])
            rv, krv = LT[2]
            stt(DVE, rv[:, 0:n2], ps[7][:, 0:n2], 1.0 / BR, msq[:, 0:n2], ALU.mult, ALU.subtract, [("ps", 7), kmsq], [krv])
            ts(DVE, rv[:, 0:n2], rv[:, 0:n2], 4e-5, None, ALU.add, None, [krv], [krv])
            P.op(ACT, lambda e: e.activation(out=rv[:, 0:n2], in_=rv[:, 0:n2], func=AF.Sqrt), [krv], [krv])
            P.op(DVE, lambda e: e.reciprocal(out=rv[:, 0:n2], in_=rv[:, 0:n2]), [krv], [krv])
            nmr, knmr = LT[3]
            tt(DVE, nmr[:, 0:n2], m[:, 0:n2], rv[:, 0:n2], ALU.mult, [km, krv], [knmr])
            for i in range(BC):
                gmlp_ln_chunk(j, i, twt, nb, rv, krv, nmr, knmr)
            for i in range(BC):
                gmlp_ugroup(j, i, twt, nb)
            for h in range(HEADS):
                mem_head(h, 2, n2, 0)

        def out_unit(jo, twt):
            n2 = twt - 2
            b = unit_matmuls(lambda kc: yT[:, kc, 2:twt], n2, ally)
            act(y2[:, jo, 2:twt], ps[b][:, 0:n2], AF.Copy, [("ps", b)], [("y2", jo)])
            t, kt = T()
            act(t[:, 0:n2], ps[b][:, 0:n2], AF.Square, [("ps", b)], [kt])

            def stats():
                mm(ps[7][:, 0:n2], ones_f[:], t[:, 0:n2], jo == 0, jo == KC - 1, [kt, "ones_f"], [("ps", 7)], True)
            return stats

        def resid_chunk(l, c, twt, rs, krs):
            n2 = twt - 2
            t, kt = T()
            stt(DVE, t[:, 0:n2], y2[:, c, 2:twt], cs(cfg.o_gpost, l * KC + c), rs[:, 0:n2], ALU.mult, ALU.mult, [("y2", c), krs, "cst"], [kt])
            tt(POOL, xT[:, c, 2:twt], xT[:, c, 2:twt], t[:, 0:n2], ALU.add, [kt, ("x", c)], [("x", c)])

        def out_proj(l, twt):
            n2 = twt - 2
            pend = None
            for jo in range(KC):
                st_ = out_unit(jo, twt)
                if pend is not None:
                    pend()
                pend = st_
            pend()
            rs, krs = LT[0]
            ts(DVE, rs[:, 0:n2], ps[7][:, 0:n2], 1.0 / D, 1e-6, ALU.mult, ALU.add, [("ps", 7)], [krs])
            P.op(ACT, lambda e: e.activation(out=rs[:, 0:n2], in_=rs[:, 0:n2], func=AF.Sqrt), [krs], [krs])
            P.op(DVE, lambda e: e.reciprocal(out=rs[:, 0:n2], in_=rs[:, 0:n2]), [krs], [krs])
            for c in range(KC):
                resid_chunk(l, c, twt, rs, krs)

        out_toks = []
        for ti, (b0, nb) in enumerate(cfg.tiles):
            twt = 2 + 128 * nb
            t0 = 128 * b0
            x_dma(lambda ca, cb, twt=twt: xT[:, ca:cb, 0:twt], lambda ca, cb, t0=t0, twt=twt: xT_v[:, ca:cb, t0:t0 + twt], True)
            for l in range(L):
                c0 = 0 if (ti == 0 and l == 0) else 2
                k_cnt[0] += 1
                P.dma(SP, lambda e, l=l: e.dma_start(out=KT[:], in_=kvs_d[l][:, 0:MC * NM]), sem_k, 16 * k_cnt[0],
                      reads=[("kvs", l, 0)], writes=["KT"])
                v_cnt[0] += 1
                P.dma(SP, lambda e, l=l: e.dma_start(out=Vt[:], in_=kvs_d[l][:, MC * NM:]), sem_v, 16 * v_cnt[0],
                      reads=[("kvs", l, 1)], writes=["Vt"])
                rmsnorm_T(xT, "x", hT, "h", cfg.o_gpre + l * KC, c0, twt)
                if l % 2 == 0:
                    conv_layer(l, c0, twt)
                else:
                    gmlp_layer(l, c0, twt, nb)
                out_proj(l, twt)
            lo = max(t0 + 2, 2 + cfg.HALO)
            hi = t0 + twt
            if hi > lo:
                out_toks += x_dma(lambda ca, cb, lo=lo, hi=hi: out_v[:, ca:cb, lo - 2 - cfg.HALO:hi - 2 - cfg.HALO],
                                  lambda ca, cb, lo=lo, hi=hi, t0=t0: xT[:, ca:cb, lo - t0:hi - t0], False)
        P.wait_all(SP, out_toks)
        P.emit()
    return nc


def prepare_inputs(cfg, x, mem, pre_norm_g, post_norm_g, mem_norm_g, w_mem_kv, w_out,
                   conv_w_in, conv_w, gmlp_w_in, gmlp_ln_g, gmlp_ln_b, gmlp_w_s, gmlp_b_s):
    D, KC, BC, MC, GG, L = cfg.D, cfg.KC, cfg.BC, cfg.MC, cfg.GG, cfg.DEPTH
    f32 = np.float32
    x = np.asarray(x, f32)[0]
    mem = np.asarray(mem, f32)[0]
    shared = {}
    shared["memT"] = np.ascontiguousarray(mem.T)
    cst = np.zeros((128, cfg.NCST), f32)

    def put(off, vec2d):
        n, W = vec2d.shape
        C = W // 128
        cst[:, off:off + n * C] = vec2d.reshape(n, C, 128).transpose(2, 0, 1).reshape(128, n * C)

    put(cfg.o_gpre, np.asarray(pre_norm_g, f32))
    put(cfg.o_gpost, np.asarray(post_norm_g, f32))
    put(cfg.o_gmem, np.asarray(mem_norm_g, f32))
    cw = np.asarray(conv_w, f32)
    put(cfg.o_cw, cw.reshape(cfg.NCONV * 3, cfg.BR))
    if cfg.NGM:
        put(cfg.o_lng, np.asarray(gmlp_ln_g, f32))
        put(cfg.o_lnb, np.asarray(gmlp_ln_b, f32))
    shared["cst"] = cst
    NBS = max(cfg.NGM, 1) * GG * 128
    cst2 = np.zeros((128, 2 * NBS + 256), f32)
    if cfg.NGM:
        bs = np.asarray(gmlp_b_s, f32).reshape(1, cfg.NGM * GG * 128)
        cst2[:, 0:NBS] = np.broadcast_to(bs, (128, NBS))
        ws = np.asarray(gmlp_w_s, f32)
        cst2[:, NBS:2 * NBS] = ws.transpose(3, 0, 1, 2).reshape(128, NBS)
    s_idx = np.arange(128)[:, None]
    t_idx = np.arange(128)[None, :]
    cst2[:, 2 * NBS:2 * NBS + 128] = (s_idx <= t_idx).astype(f32)
    cst2[:, 2 * NBS + 128:2 * NBS + 256] = np.eye(128, dtype=f32)
    shared["cst2"] = cst2
    for l in range(L):
        units = cfg.layer_units(l)
        chunks = [u[2] for u in units]
        Win = np.asarray(conv_w_in[l // 2] if l % 2 == 0 else gmlp_w_in[l // 2], f32)
        a = _units_from(Win, chunks, KC)
        b = _units_from(np.asarray(w_out[l], f32), list(range(KC)), KC)
        shared[f"w{l}"] = np.concatenate([a, b], axis=0)
    shared["wkv"] = np.concatenate([_units_from(np.asarray(w_mem_kv[l], f32), list(range(2 * MC)), KC) for l in range(L)], axis=0)
    in_maps = []
    pre = 2 + cfg.HALO
    for c in range(cfg.NCORES):
        s = c * cfg.TOWN
        xl = np.zeros((cfg.TLOC, D), f32)
        lo = s - pre
        if lo >= 0:
            xl[:] = x[lo:s + cfg.TOWN]
        else:
            xl[-lo:] = x[0:s + cfg.TOWN]
        m = dict(shared)
        m["xT"] = np.ascontiguousarray(xl.T)
        in_maps.append(m)
    return in_maps


_CACHE = {}


def kernel(x, mem, pre_norm_g, post_norm_g, mem_norm_g, w_mem_kv, w_out,
           conv_w_in, conv_w, gmlp_w_in, gmlp_ln_g, gmlp_ln_b, gmlp_w_s, gmlp_b_s):
    cfg = Cfg()
    in_maps = prepare_inputs(cfg, x, mem, pre_norm_g, post_norm_g, mem_norm_g, w_mem_kv, w_out,
                             conv_w_in, conv_w, gmlp_w_in, gmlp_ln_g, gmlp_ln_b, gmlp_w_s, gmlp_b_s)
    if "nc" not in _CACHE:
        _CACHE["nc"] = build_program(cfg)
    res = run_bass_kernel_spmd(_CACHE["nc"], in_maps, core_ids=list(range(cfg.NCORES)))
    outs = [np.asarray(r["outT"]).T for r in res.results]
    return np.ascontiguousarray(np.concatenate(outs, axis=0)[None]).astype(np.float32)
```

```python
import numpy as np
from contextlib import ExitStack
import concourse.bass as bass
import concourse.mybir as mybir
from concourse.bass_utils import run_bass_kernel_spmd

F32 = mybir.dt.float32
BF16 = mybir.dt.bfloat16
AF = mybir.ActivationFunctionType
ALU = mybir.AluOpType

PE, ACT, DVE, POOL, SP = "tensor", "scalar", "vector", "gpsimd", "sync"
ENGINES = [PE, ACT, DVE, POOL, SP]


class Tok:
    __slots__ = ("sem", "val")

    def __init__(self, sem, val=None):
        self.sem = sem
        self.val = val


class Prog:
    def __init__(self, nc, stack):
        self.nc = nc
        self.stack = stack
        self.q = {e: [] for e in ENGINES}
        self.esem = {}
        for e in (PE, ACT, DVE, POOL):
            self.esem[e] = stack.enter_context(nc.semaphore("es_" + e))
        self.cnt = {e: 0 for e in (PE, ACT, DVE, POOL)}
        self.pending_pe = []
        self.last_w = {}
        self.readers = {}
        self.nsem = 0

    def new_sem(self, name=None):
        self.nsem += 1
        return self.stack.enter_context(self.nc.semaphore(name or f"s{self.nsem}"))

    def _deps(self, reads, writes):
        deps = []
        for k in reads:
            t = self.last_w.get(k)
            if t is not None:
                deps.append(t)
        for k in writes:
            t = self.last_w.get(k)
            if t is not None:
                deps.append(t)
            deps.extend(self.readers.get(k, ()))
        return deps

    def _commit(self, tok, reads, writes):
        for k in reads:
            self.readers.setdefault(k, []).append(tok)
        for k in writes:
            self.last_w[k] = tok
            self.readers[k] = []

    def op(self, eng, fn, reads=(), writes=(), inc=True):
        deps = self._deps(reads, writes)
        if eng == PE:
            deps = [t for t in deps if t.sem is not self.esem[PE]]
            tok = Tok(self.esem[PE])
            self.pending_pe.append(tok)
            if inc:
                self.cnt[PE] += 1
                for t in self.pending_pe:
                    t.val = self.cnt[PE]
                self.pending_pe = []
        else:
            self.cnt[eng] += 1
            tok = Tok(self.esem[eng], self.cnt[eng])
        self._commit(tok, reads, writes)
        self.q[eng].append((deps, fn, (self.esem[eng], 1) if inc else None, tok))
        return tok

    def dma(self, eng, fn, sem, semval, reads=(), writes=()):
        deps = self._deps(reads, writes)
        tok = Tok(sem, semval)
        self._commit(tok, reads, writes)
        self.q[eng].append((deps, fn, (sem, 16), tok))
        return tok

    def wait_all(self, eng, toks):
        self.q[eng].append((list(toks), None, None, None))

    def emit(self):
        nc = self.nc
        assert not self.pending_pe
        with nc.Block() as block:
            for ename in ENGINES:
                items = self.q[ename]

                def body(eng, items=items):
                    seen = {}
                    for deps, fn, inc, tok in items:
                        need = {}
                        for t in deps:
                            if t is tok:
                                continue
                            sid = id(t.sem)
                            if t.val > seen.get(sid, 0):
                                if sid not in need or need[sid][1] < t.val:
                                    need[sid] = (t.sem, t.val)
                        for sid, (s, v) in need.items():
                            eng.wait_ge(s, v)
                            seen[sid] = v
                        if fn is not None:
                            ins = fn(eng)
                            if inc is not None:
                                ins.then_inc(inc[0], inc[1])

                getattr(block, ename)(body)


class Cfg:
    def __init__(self, D=4096, BR=3072, HEADS=4, GG=8, NMEM=256, DEPTH=4, TOWN=2048,
                 TILE_BLOCKS=2, NCORES=8, R=6):
        self.D, self.BR, self.HEADS, self.GG, self.NMEM, self.DEPTH = D, BR, HEADS, GG, NMEM, DEPTH
        self.TOWN, self.NCORES, self.R = TOWN, NCORES, R
        self.MW = HEADS * 256
        assert D == BR + self.MW
        self.KC = D // 128
        self.BC = BR // 128
        self.MC = self.MW // 128
        self.GCH = self.BC // GG
        assert self.GCH * GG == self.BC
        self.HALO = 128
        self.TLOC = 2 + self.HALO + TOWN
        nblocks = (self.HALO + TOWN) // 128
        self.tiles = []
        b = 0
        while b < nblocks:
            nb = min(TILE_BLOCKS, nblocks - b)
            self.tiles.append((b, nb))
            b += nb
        self.NBMAX = TILE_BLOCKS
        self.TW = 2 + 128 * TILE_BLOCKS
        self.NCONV = (DEPTH + 1) // 2
        self.NGM = DEPTH // 2
        self.CONV_IN = 3 * BR + self.MW + D
        self.GM_IN = 2 * BR + self.MW + D
        o = 0
        self.o_gpre = o; o += DEPTH * self.KC
        self.o_gpost = o; o += DEPTH * self.KC
        self.o_gmem = o; o += DEPTH * self.KC
        self.o_cw = o; o += self.NCONV * 3 * self.BC
        self.o_lng = o; o += max(self.NGM, 1) * self.BC
        self.o_lnb = o; o += max(self.NGM, 1) * self.BC
        self.NCST = o

    def layer_units(self, l):
        BC, MC = self.BC, self.MC
        u = []
        if l % 2 == 0:
            oB, oC, oH, oQ, oZ = 0, BC, 2 * BC, 3 * BC, 3 * BC + MC
            for i in range(BC):
                u += [("C", i, oC + i), ("H", i, oH + i), ("B", i, oB + i), ("Z", i, oZ + i)]
        else:
            oU, oV, oQ, oZ = 0, BC, 2 * BC, 2 * BC + MC
            for i in range(BC):
                u.append(("V", i, oV + i))
            for i in range(BC):
                u += [("U", i, oU + i), ("Z", i, oZ + i)]
        m = []
        for h in range(self.HEADS):
            m += [("Q", 2 * h, oQ + 2 * h), ("Q", 2 * h + 1, oQ + 2 * h + 1),
                  ("Z", BC + 2 * h, oZ + BC + 2 * h), ("Z", BC + 2 * h + 1, oZ + BC + 2 * h + 1)]
        return m + u


def _units_from(W, chunks, KC):
    K, NC = W.shape
    Wr = W.reshape(KC, 128, NC // 128, 128).transpose(2, 1, 0, 3)
    return np.ascontiguousarray(Wr[np.asarray(chunks)]).reshape(len(chunks), 128, KC * 128)


def build_program(cfg):
    nc = bass.Bass("TRN2", target_bir_lowering=False)
    D, KC, BC, MC, HEADS, GG, GCH = cfg.D, cfg.KC, cfg.BC, cfg.MC, cfg.HEADS, cfg.GG, cfg.GCH
    L, TW, NM, R, BR = cfg.DEPTH, cfg.TW, cfg.NMEM, cfg.R, cfg.BR
    assert NM == 256 and TW >= NM
    WU = KC * 128
    NBS = max(cfg.NGM, 1) * GG * 128
    xT_d = nc.dram_tensor("xT", [D, cfg.TLOC], F32, kind="ExternalInput").ap()
    memT_d = nc.dram_tensor("memT", [D, NM], F32, kind="ExternalInput").ap()
    cst_d = nc.dram_tensor("cst", [128, cfg.NCST], F32, kind="ExternalInput").ap()
    cst2_d = nc.dram_tensor("cst2", [128, 2 * NBS + 256], F32, kind="ExternalInput").ap()
    w_d = []
    for l in range(L):
        n_u = len(cfg.layer_units(l)) + KC
        w_d.append(nc.dram_tensor(f"w{l}", [n_u, 128, WU], F32, kind="ExternalInput").ap())
    wkv_d = nc.dram_tensor("wkv", [L * 2 * MC, 128, WU], F32, kind="ExternalInput").ap()
    out_d = nc.dram_tensor("outT", [D, cfg.TOWN], F32, kind="ExternalOutput").ap()
    kvs_d = nc.dram_tensor("kvs", [L, 128, MC * NM + 2 * cfg.MW], BF16).ap()
    wb_d = [nc.dram_tensor(f"wb{l}", [len(cfg.layer_units(l)) + KC, 128, WU], BF16).ap() for l in range(L)]
    NKV = L * 2 * cfg.MC

    xT_v = xT_d.rearrange("(c p) t -> p c t", p=128)
    out_v = out_d.rearrange("(c p) t -> p c t", p=128)
    memT_v = memT_d.rearrange("(c p) t -> p c t", p=128)

    with ExitStack() as st:
        P = Prog(nc, st)

        def sb(name, shape, dt):
            return st.enter_context(nc.sbuf_tensor(name, shape, dt))

        xT = sb("xT_sb", [128, KC, TW], F32)
        hT = sb("hT_sb", [128, KC, TW], BF16)
        yT = sb("yT_sb", [128, KC, TW], BF16)
        y2 = sb("y2_sb", [128, KC, TW], F32)
        wring = [sb(f"wr{i}", [128, WU], BF16) for i in range(R)]
        wsem = [P.new_sem(f"wsem{i}") for i in range(R)]
        vln = sb("vln", [128, cfg.NBMAX, BC * 128], BF16)
        qT = sb("qT", [128, 4, TW], BF16)
        pT = sb("pT", [128, 2, TW], BF16)
        KT = sb("KT", [128, MC * NM], BF16)
        Vt = sb("Vt", [128, 2 * cfg.MW], BF16)
        cst = sb("cst_sb", [128, cfg.NCST], F32)
        bsb = sb("bsb", [128, NBS], F32)
        wsTm = sb("wsTm", [128, NBS], BF16)
        stg = sb("stg", [128, 256], F32)
        ones_f = sb("ones_f", [128, 128], F32)
        ones_b = sb("ones_b", [128, 128], BF16)
        cstate = sb("cstate", [128, max(cfg.NCONV, 1) * BC * 2], F32)
        NT = 14
        tmp = [sb(f"tmp{i}", [128, TW], F32) for i in range(NT)]
        ps = [st.enter_context(nc.psum_tensor(f"ps{i}", [128, 512], F32)) for i in range(8)]
        LT = [(sb(f"lt{i}", [128, TW], F32), ("lt", i)) for i in range(4)]
        NTB = 6
        tbt = [sb(f"tb{i}", [128, TW], BF16) for i in range(NTB)]
        tbctr = [0]

        def TB():
            i = tbctr[0] % NTB
            tbctr[0] += 1
            return tbt[i], ("tb", i)
        ident = stg[:, 128:256]
        maskT = stg[:, 0:128]

        sem_misc = P.new_sem("misc")
        misc_cnt = [0]
        XG = 4
        NXG = (KC + XG - 1) // XG
        sem_xg = [P.new_sem(f"xio{g}") for g in range(NXG)]
        xg_cnt = [0] * NXG

        def x_dma(out_fn, in_fn, is_load):
            toks = []
            for g in range(NXG):
                ca, cb = g * XG, min(KC, (g + 1) * XG)
                xg_cnt[g] += 1
                keys = [("x", c) for c in range(ca, cb)]
                o, i_ = out_fn(ca, cb), in_fn(ca, cb)
                toks.append(P.dma(SP, lambda e, o=o, i_=i_: e.dma_start(out=o, in_=i_), sem_xg[g], 16 * xg_cnt[g],
                                  reads=([] if is_load else keys), writes=(keys if is_load else [])))
            return toks
        sem_k = P.new_sem("kio")
        sem_v = P.new_sem("vio")
        k_cnt = [0]
        v_cnt = [0]

        def dma_misc(eng, out, in_, reads=(), writes=()):
            misc_cnt[0] += 1
            sm = P.new_sem(f"misc{misc_cnt[0]}")
            return P.dma(eng, lambda e: e.dma_start(out=out, in_=in_), sm, 16, reads, writes)

        tctr = [0]

        def T():
            i = tctr[0] % NT
            tctr[0] += 1
            return tmp[i], ("tmp", i)

        bctr = [0]

        def bank():
            b = bctr[0] % 6
            bctr[0] += 1
            return b

        wq = []
        wn = [0]
        wcnt = [0] * R

        wbsem = [P.new_sem(f"wbsem{i}") for i in range(R)]
        wbcnt = [0] * R
        wsem2 = [P.new_sem(f"wsemh{i}") for i in range(R)]
        wcnt2 = [0] * R

        def w_issue(idx):
            src, scr, mode = wq[idx]
            s = idx % R
            wcnt[s] += 1
            if mode == 2:
                wcnt[s] -= 1
                wcnt2[s] += 1
                P.dma(SP, lambda e, s=s, scr=scr: e.dma_start(out=wring[s][:], in_=scr), wsem2[s],
                      16 * wcnt2[s], reads=[("wb", (idx - NKV) % UPT)], writes=[("w", s)])
                return
            P.dma(POOL, lambda e, s=s, src=src: e.dma_start(out=wring[s][:], in_=src), wsem[s],
                  16 * wcnt[s], writes=[("w", s)])
            if mode == 1:
                wbcnt[s] += 1
                P.dma(SP, lambda e, s=s, scr=scr: e.dma_start(out=scr, in_=wring[s][:]), wbsem[s],
                      16 * wbcnt[s], reads=[("w", s)], writes=[("wb", (idx - NKV) % UPT)])

        def unit_matmuls(rhs_fn, n, reads, bnk=None):
            i = wn[0]
            wn[0] += 1
            s = i % R
            b = bank() if bnk is None else bnk
            for kc in range(KC):
                rhs = rhs_fn(kc)
                P.op(PE, lambda e, s=s, b=b, kc=kc, rhs=rhs: e.matmul(ps[b][:, 0:n], lhsT=wring[s][:, kc * 128:(kc + 1) * 128],
                                                                     rhs=rhs, start=(kc == 0), stop=(kc == KC - 1)),
                     reads=[("w", s)] + reads, writes=[("ps", b)], inc=(kc == KC - 1))
            if i + R < len(wq):
                w_issue(i + R)
            return b

        NKV = L * 2 * MC
        UPT = sum(len(cfg.layer_units(l)) + KC for l in range(L))
        for l in range(L):
            for u in range(2 * MC):
                wq.append((wkv_d[l * 2 * MC + u], None, 0))
        assert NKV < UPT
        KVPAD = NKV
        multi = len(cfg.tiles) > 1
        for ti_, (b0, nb) in enumerate(cfg.tiles):
            for l in range(L):
                for u in range(len(cfg.layer_units(l)) + KC):
                    if not multi:
                        wq.append((w_d[l][u], None, 0))
                    elif ti_ == 0:
                        wq.append((w_d[l][u], wb_d[l][u], 1))
                    else:
                        wq.append((None, wb_d[l][u], 2))

        dma_misc(SP, cst[:], cst_d[:, :], writes=["cst"])
        dma_misc(SP, bsb[:], cst2_d[:, 0:NBS], writes=["bsb"])
        dma_misc(SP, stg[:], cst2_d[:, 2 * NBS:2 * NBS + 256], writes=["stg"])
        P.op(DVE, lambda e: e.memset(ones_f[:], 1.0), writes=["ones_f"])
        P.op(DVE, lambda e: e.memset(ones_b[:], 1.0), writes=["ones_b"])
        P.op(DVE, lambda e: e.memset(cstate[:], 0.0), writes=[("cstate", j_, i_) for j_ in range(max(cfg.NCONV, 1)) for i_ in range(BC)])
        for gidx in range(cfg.NGM * GG):
            t, kt = T()
            dma_misc(SP, t[:, 0:128], cst2_d[:, NBS + gidx * 128:NBS + (gidx + 1) * 128], writes=[kt])
            P.op(DVE, lambda e, t=t, gidx=gidx: e.tensor_tensor(out=wsTm[:, gidx * 128:(gidx + 1) * 128], in0=t[:, 0:128],
                                                                 in1=maskT, op=ALU.mult),
                 reads=[kt, "stg"], writes=["wsTm"])
        for i in range(min(R, len(wq))):
            w_issue(i)

        def cs(off, idx):
            return cst[:, off + idx:off + idx + 1]

        def rmsnorm_T(src, ksrc, dst, kdst, goff, c0, c1, eps=1e-6):
            n = c1 - c0
            for c in range(KC):
                t, kt = TB()
                P.op(ACT, lambda e, t=t, c=c: e.activation(out=t[:, 0:n], in_=src[:, c, c0:c1], func=AF.Square),
                     reads=[(ksrc, c)], writes=[kt])
                P.op(PE, lambda e, t=t, c=c: e.matmul(ps[7][:, 0:n], lhsT=ones_b[:], rhs=t[:, 0:n], start=(c == 0),
                                                     stop=(c == KC - 1)),
                     reads=[kt, "ones_b"], writes=[("ps", 7)], inc=True)
            rs, krs = LT[0]
            P.op(DVE, lambda e: e.tensor_scalar(out=rs[:, 0:n], in0=ps[7][:, 0:n], scalar1=1.0 / D, scalar2=eps,
                                                op0=ALU.mult, op1=ALU.add), reads=[("ps", 7)], writes=[krs])
            P.op(ACT, lambda e: e.activation(out=rs[:, 0:n], in_=rs[:, 0:n], func=AF.Sqrt), reads=[krs], writes=[krs])
            P.op(DVE, lambda e: e.reciprocal(out=rs[:, 0:n], in_=rs[:, 0:n]), reads=[krs], writes=[krs])
            for c in range(KC):
                eng = DVE
                P.op(eng, lambda e, c=c: e.scalar_tensor_tensor(out=dst[:, c, c0:c1], in0=src[:, c, c0:c1],
                                                                 scalar=cs(goff, c), in1=rs[:, 0:n],
                                                                 op0=ALU.mult, op1=ALU.mult),
                     reads=[(ksrc, c), krs, "cst"], writes=[(kdst, c)])

        allh = [("h", kc) for kc in range(KC)]
        ally = [("y", kc) for kc in range(KC)]

        x_dma(lambda ca, cb: xT[:, ca:cb, 0:NM], lambda ca, cb: memT_v[:, ca:cb, :], True)
        for l in range(L):
            rmsnorm_T(xT, "x", hT, "h", cfg.o_gmem + l * KC, 0, NM)
            for u in range(2 * MC):
                b = unit_matmuls(lambda kc: hT[:, kc, 0:NM], NM, allh)
                if u < MC:
                    P.op(ACT, lambda e, b=b, u=u: e.activation(out=KT[:, u * NM:(u + 1) * NM], in_=ps[b][:, 0:NM], func=AF.Copy),
                         reads=[("ps", b)], writes=["KT"])
                else:
                    vc = u - MC
                    t, kt = T()
                    P.op(ACT, lambda e, b=b, t=t: e.activation(out=t[:, 0:NM], in_=ps[b][:, 0:NM], func=AF.Copy),
                         reads=[("ps", b)], writes=[kt])
                    for mc in range(2):
                        b2 = bank()
                        P.op(PE, lambda e, b2=b2, mc=mc, t=t: e.transpose(out=ps[b2][:, 0:128], in_=t[:, mc * 128:(mc + 1) * 128],
                                                                         identity=ident),
                             reads=[kt, "stg"], writes=[("ps", b2)])
                        P.op(DVE, lambda e, b2=b2, mc=mc, vc=vc: e.tensor_copy(
                            out=Vt[:, mc * cfg.MW + vc * 128:mc * cfg.MW + (vc + 1) * 128], in_=ps[b2][:, 0:128]),
                             reads=[("ps", b2)], writes=["Vt"])
            k_cnt[0] += 1
            P.dma(SP, lambda e, l=l: e.dma_start(out=kvs_d[l][:, 0:MC * NM], in_=KT[:]),
                  sem_k, 16 * k_cnt[0], reads=["KT"], writes=[("kvs", l, 0)])
            v_cnt[0] += 1
            P.dma(SP, lambda e, l=l: e.dma_start(out=kvs_d[l][:, MC * NM:], in_=Vt[:]),
                  sem_v, 16 * v_cnt[0], reads=["Vt"], writes=[("kvs", l, 1)])

        def act(out, in_, func, reads, writes, scale=None):
            if scale is None:
                P.op(ACT, lambda e: e.activation(out=out, in_=in_, func=func), reads, writes)
            else:
                P.op(ACT, lambda e: e.activation(out=out, in_=in_, func=func, scale=scale), reads, writes)

        def tt(eng, out, in0, in1, op, reads, writes):
            P.op(eng, lambda e: e.tensor_tensor(out=out, in0=in0, in1=in1, op=op), reads, writes)

        def ts(eng, out, in0, s1, s2, op0, op1, reads, writes):
            if s2 is None:
                P.op(eng, lambda e: e.tensor_scalar(out=out, in0=in0, scalar1=s1, scalar2=None, op0=op0), reads, writes)
            else:
                P.op(eng, lambda e: e.tensor_scalar(out=out, in0=in0, scalar1=s1, scalar2=s2, op0=op0, op1=op1), reads, writes)

        def stt(eng, out, in0, scalar, in1, op0, op1, reads, writes):
            P.op(eng, lambda e: e.scalar_tensor_tensor(out=out, in0=in0, scalar=scalar, in1=in1, op0=op0, op1=op1), reads, writes)

        def cp(eng, out, in_, reads, writes):
            P.op(eng, lambda e: e.tensor_copy(out=out, in_=in_), reads, writes)

        def mm(out, lhsT, rhs, start, stop, reads, writes, inc):
            P.op(PE, lambda e: e.matmul(out, lhsT=lhsT, rhs=rhs, start=start, stop=stop), reads, writes, inc=inc)

        def tr(out, in_, reads, writes):
            P.op(PE, lambda e: e.transpose(out=out, in_=in_, identity=ident), reads, writes)

        GC = 0.7978845608028654

        def gelu_w(b, n, dst_ap, kdst):
            src = ps[b][:, 0:n]
            kb = ("ps", b)
            t1, k1 = T()
            act(t1[:, 0:n], src, AF.Square, [kb], [k1])
            ts(DVE, t1[:, 0:n], t1[:, 0:n], 0.044715, 1.0, ALU.mult, ALU.add, [k1], [k1])
            tt(DVE, t1[:, 0:n], src, t1[:, 0:n], ALU.mult, [k1, kb], [k1])
            act(t1[:, 0:n], t1[:, 0:n], AF.Tanh, [k1], [k1], scale=GC)
            stt(DVE, dst_ap, t1[:, 0:n], 1.0, src, ALU.add, ALU.mult, [k1, kb], [kdst])

        def silu2(b, o0, n, eng=DVE):
            src = ps[b][:, o0:o0 + n]
            kb = ("ps", b)
            t1, k1 = T()
            act(t1[:, 0:n], src, AF.Tanh, [kb], [k1], scale=0.5)
            stt(eng, t1[:, 0:n], t1[:, 0:n], 1.0, src, ALU.add, ALU.mult, [k1, kb], [k1])
            return t1, k1

        def mem_units(h, c0, n2, o0):
            n = n2 + o0
            hp = h % 2
            rhsf = lambda kc: hT[:, kc, c0:c0 + n]
            bq = [unit_matmuls(rhsf, n, allh) for _ in range(2)]
            for dc in range(2):
                act(qT[:, 2 * hp + dc, 0:n2], ps[bq[dc]][:, o0:o0 + n2], AF.Copy, [("ps", bq[dc])], [("qT", 2 * hp + dc)])
            bz = [unit_matmuls(rhsf, n, allh) for _ in range(2)]
            return [silu2(bz[dc], o0, n2) for dc in range(2)]

        def mem_all(c0, n2, o0):
            prev = None
            for h in range(HEADS):
                cur = (h, mem_units(h, c0, n2, o0))
                if prev is not None:
                    mem_attn(prev[0], prev[1], n2)
                prev = cur
            mem_attn(prev[0], prev[1], n2)

        def mem_attn(h, szm, n2):
            hp = h % 2
            for mc in range(2):
                b = bank()
                for dc in range(2):
                    o = (2 * h + dc) * NM + mc * 128
                    mm(ps[b][:, 0:n2], KT[:, o:o + 128], qT[:, 2 * hp + dc, 0:n2], dc == 0, dc == 1,
                       ["KT", ("qT", 2 * hp + dc)], [("ps", b)], dc == 1)
                act(pT[:, mc, 0:n2], ps[b][:, 0:n2], AF.Exp, [("ps", b)], [("pT", mc)], scale=1.0 / 16.0)
            bd = bank()
            for mc in range(2):
                mm(ps[bd][:, 0:n2], ones_b[:], pT[:, mc, 0:n2], mc == 0, mc == 1, ["ones_b", ("pT", mc)], [("ps", bd)], mc == 1)
            rd, krd = T()
            P.op(DVE, lambda e: e.reciprocal(out=rd[:, 0:n2], in_=ps[bd][:, 0:n2]), [("ps", bd)], [krd])
            for dc in range(2):
                b = bank()
                for mc in range(2):
                    o = mc * cfg.MW + h * 256 + dc * 128
                    mm(ps[b][:, 0:n2], Vt[:, o:o + 128], pT[:, mc, 0:n2], mc == 0, mc == 1, ["Vt", ("pT", mc)], [("ps", b)], mc == 1)
                t, kt = T()
                stt(DVE, t[:, 0:n2], ps[b][:, 0:n2], 0.5, rd[:, 0:n2], ALU.mult, ALU.mult, [("ps", b), krd], [kt])
                sz, ksz = szm[dc]
                ch = BC + 2 * h + dc
                tt(POOL, yT[:, ch, 2:2 + n2], t[:, 0:n2], sz[:, 0:n2], ALU.mult, [kt, ksz], [("y", ch)])

        def conv_group(l, i, c0, twt):
            j = l // 2
            n = twt - c0
            n2 = twt - 2
            o0 = 2 - c0
            cwo = cfg.o_cw + j * 3 * BC
            rhsf = lambda kc: hT[:, kc, c0:twt]
            bC = unit_matmuls(rhsf, n, allh)
            bH = unit_matmuls(rhsf, n, allh)
            bB = unit_matmuls(rhsf, n, allh)
            bZ = unit_matmuls(rhsf, n, allh)
            tc_, ktc = T()
            act(tc_[:, 0:n], ps[bC][:, 0:n], AF.Copy, [("ps", bC)], [ktc])
            g, kg = T()
            so = (j * BC + i) * 2
            kst = ("cstate", j, i)
            if c0 == 2:
                cp(POOL, g[:, 0:2], cstate[:, so:so + 2], [kst], [kg])
            tt(DVE, g[:, c0:twt], ps[bH][:, 0:n], tc_[:, 0:n], ALU.mult, [("ps", bH), ktc], [kg])
            cp(POOL, cstate[:, so:so + 2], g[:, twt - 2:twt], [kg], [kst])
            acc, ka = T()
            ts(DVE, acc[:, 0:n2], g[:, 0:n2], cs(cwo, 0 * BC + i), None, ALU.mult, None, [kg, "cst"], [ka])
            stt(DVE, acc[:, 0:n2], g[:, 1:1 + n2], cs(cwo, 1 * BC + i), acc[:, 0:n2], ALU.mult, ALU.add, [kg, ka, "cst"], [ka])
            stt(DVE, acc[:, 0:n2], g[:, 2:2 + n2], cs(cwo, 2 * BC + i), acc[:, 0:n2], ALU.mult, ALU.add, [kg, ka, "cst"], [ka])
            stt(DVE, acc[:, 0:n2], ps[bB][:, o0:o0 + n2], 0.5, acc[:, 0:n2], ALU.mult, ALU.mult, [("ps", bB), ka], [ka])
            sz, ksz = silu2(bZ, o0, n2)
            tt(POOL, yT[:, i, 2:2 + n2], acc[:, 0:n2], sz[:, 0:n2], ALU.mult, [ka, ksz], [("y", i)])

        def conv_layer(l, c0, twt):
            mem_all(c0, twt - 2, 2 - c0)
            for i in range(BC):
                conv_group(l, i, c0, twt)

        def gmlp_vunit(i, twt, first, last):
            n2 = twt - 2
            b = unit_matmuls(lambda kc: hT[:, kc, 2:twt], n2, allh)
            gelu_w(b, n2, y2[:, i, 2:twt], ("y2", i))
            t, kt = TB()
            act(t[:, 0:n2], y2[:, i, 2:twt], AF.Square, [("y2", i)], [kt])
            t2, kt2 = TB()
            act(t2[:, 0:n2], y2[:, i, 2:twt], AF.Copy, [("y2", i)], [kt2])

            def stats():
                mm(ps[6][:, 0:n2], ones_b[:], t2[:, 0:n2], first, last, [kt2, "ones_b"], [("ps", 6)], True)
                mm(ps[7][:, 0:n2], ones_b[:], t[:, 0:n2], first, last, [kt, "ones_b"], [("ps", 7)], True)
            return stats

        def gmlp_ln_chunk(j, i, twt, nb, rv, krv, nmr, knmr):
            n2 = twt - 2
            t, kt = T()
            tt(DVE, t[:, 0:n2], y2[:, i, 2:twt], rv[:, 0:n2], ALU.mult, [("y2", i), krv], [kt])
            tt(POOL, t[:, 0:n2], t[:, 0:n2], nmr[:, 0:n2], ALU.subtract, [kt, knmr], [kt])
            ts(DVE, t[:, 0:n2], t[:, 0:n2], cs(cfg.o_lng, j * BC + i), cs(cfg.o_lnb, j * BC + i), ALU.mult, ALU.add, [kt, "cst"], [kt])
            for blk in range(nb):
                b2 = bank()
                tr(ps[b2][:, 0:128], t[:, blk * 128:(blk + 1) * 128], [kt, "stg"], [("ps", b2)])
                act(vln[:, blk, i * 128:(i + 1) * 128], ps[b2][:, 0:128], AF.Copy, [("ps", b2)], [("vln", blk, i)])

        def gmlp_uz_units(twt):
            n2 = twt - 2
            rhsf = lambda kc: hT[:, kc, 2:twt]
            bU = unit_matmuls(rhsf, n2, allh)
            bZ = unit_matmuls(rhsf, n2, allh)
            return bU, bZ

        def gmlp_ugroup(j, i, twt, nb, bU, bZ):
            n2 = twt - 2
            ug, kug = T()
            gelu_w(bU, n2, ug[:, 0:n2], kug)
            gi = j * GG + i // GCH
            bF = bank()
            for blk in range(nb):
                mm(ps[bF][:, blk * 128:(blk + 1) * 128], vln[:, blk, i * 128:(i + 1) * 128], wsTm[:, gi * 128:(gi + 1) * 128],
                   True, True, [("vln", blk, i), "wsTm"], [("ps", bF)], blk == nb - 1)
            fb, kfb = T()
            for blk in range(nb):
                tt(DVE, fb[:, blk * 128:(blk + 1) * 128], ps[bF][:, blk * 128:(blk + 1) * 128], bsb[:, gi * 128:(gi + 1) * 128],
                   ALU.add, [("ps", bF), "bsb"], [kfb])
            stt(DVE, fb[:, 0:n2], fb[:, 0:n2], 0.25, ug[:, 0:n2], ALU.mult, ALU.mult, [kfb, kug], [kfb])
            sz, ksz = silu2(bZ, 0, n2)
            tt(POOL, yT[:, i, 2:2 + n2], fb[:, 0:n2], sz[:, 0:n2], ALU.mult, [kfb, ksz], [("y", i)])

        def gmlp_layer(l, c0, twt, nb):
            j = l // 2
            n2 = twt - 2
            assert c0 == 2
            mem_all(2, n2, 0)
            pend = None
            for i in range(BC):
                st_ = gmlp_vunit(i, twt, i == 0, i == BC - 1)
                if pend is not None:
                    pend()
                pend = st_
            pend()
            m, km = LT[1]
            ts(DVE, m[:, 0:n2], ps[6][:, 0:n2], 1.0 / BR, None, ALU.mult, None, [("ps", 6)], [km])
            msq, kmsq = T()
            tt(DVE, msq[:, 0:n2], m[:, 0:n2], m[:, 0:n2], ALU.mult, [km], [kmsq])
            rv, krv = LT[2]
            stt(DVE, rv[:, 0:n2], ps[7][:, 0:n2], 1.0 / BR, msq[:, 0:n2], ALU.mult, ALU.subtract, [("ps", 7), kmsq], [krv])
            ts(DVE, rv[:, 0:n2], rv[:, 0:n2], 4e-5, None, ALU.add, None, [krv], [krv])
            P.op(ACT, lambda e: e.activation(out=rv[:, 0:n2], in_=rv[:, 0:n2], func=AF.Sqrt), [krv], [krv])
            P.op(DVE, lambda e: e.reciprocal(out=rv[:, 0:n2], in_=rv[:, 0:n2]), [krv], [krv])
            nmr, knmr = LT[3]
            tt(DVE, nmr[:, 0:n2], m[:, 0:n2], rv[:, 0:n2], ALU.mult, [km, krv], [knmr])
            gmlp_ln_chunk(j, 0, twt, nb, rv, krv, nmr, knmr)
            for i in range(BC):
                bU, bZ = gmlp_uz_units(twt)
                if i + 1 < BC:
                    gmlp_ln_chunk(j, i + 1, twt, nb, rv, krv, nmr, knmr)
                gmlp_ugroup(j, i, twt, nb, bU, bZ)

        def out_unit(jo, twt):
            n2 = twt - 2
            b = unit_matmuls(lambda kc: yT[:, kc, 2:twt], n2, ally)
            act(y2[:, jo, 2:twt], ps[b][:, 0:n2], AF.Copy, [("ps", b)], [("y2", jo)])
            t, kt = TB()
            act(t[:, 0:n2], ps[b][:, 0:n2], AF.Square, [("ps", b)], [kt])

            def stats():
                mm(ps[7][:, 0:n2], ones_b[:], t[:, 0:n2], jo == 0, jo == KC - 1, [kt, "ones_b"], [("ps", 7)], True)
            return stats

        def resid_chunk(l, c, twt, rs, krs):
            n2 = twt - 2
            t, kt = T()
            stt(DVE, t[:, 0:n2], y2[:, c, 2:twt], cs(cfg.o_gpost, l * KC + c), rs[:, 0:n2], ALU.mult, ALU.mult, [("y2", c), krs, "cst"], [kt])
            tt(POOL if c % 2 == 0 else DVE, xT[:, c, 2:twt], xT[:, c, 2:twt], t[:, 0:n2], ALU.add, [kt, ("x", c)], [("x", c)])

        def out_proj(l, twt):
            n2 = twt - 2
            pend = None
            for jo in range(KC):
                st_ = out_unit(jo, twt)
                if pend is not None:
                    pend()
                pend = st_
            pend()
            rs, krs = LT[0]
            ts(DVE, rs[:, 0:n2], ps[7][:, 0:n2], 1.0 / D, 1e-6, ALU.mult, ALU.add, [("ps", 7)], [krs])
            P.op(ACT, lambda e: e.activation(out=rs[:, 0:n2], in_=rs[:, 0:n2], func=AF.Sqrt), [krs], [krs])
            P.op(DVE, lambda e: e.reciprocal(out=rs[:, 0:n2], in_=rs[:, 0:n2]), [krs], [krs])
            for c in range(KC):
                resid_chunk(l, c, twt, rs, krs)

        out_toks = []
        for ti, (b0, nb) in enumerate(cfg.tiles):
            twt = 2 + 128 * nb
            t0 = 128 * b0
            x_dma(lambda ca, cb, twt=twt: xT[:, ca:cb, 0:twt], lambda ca, cb, t0=t0, twt=twt: xT_v[:, ca:cb, t0:t0 + twt], True)
            for l in range(L):
                c0 = 0 if (ti == 0 and l == 0) else 2
                k_cnt[0] += 1
                P.dma(SP, lambda e, l=l: e.dma_start(out=KT[:], in_=kvs_d[l][:, 0:MC * NM]), sem_k, 16 * k_cnt[0],
                      reads=[("kvs", l, 0)], writes=["KT"])
                v_cnt[0] += 1
                P.dma(SP, lambda e, l=l: e.dma_start(out=Vt[:], in_=kvs_d[l][:, MC * NM:]), sem_v, 16 * v_cnt[0],
                      reads=[("kvs", l, 1)], writes=["Vt"])
                rmsnorm_T(xT, "x", hT, "h", cfg.o_gpre + l * KC, c0, twt)
                if l % 2 == 0:
                    conv_layer(l, c0, twt)
                else:
                    gmlp_layer(l, c0, twt, nb)
                out_proj(l, twt)
            lo = max(t0 + 2, 2 + cfg.HALO)
            hi = t0 + twt
            if hi > lo:
                out_toks += x_dma(lambda ca, cb, lo=lo, hi=hi: out_v[:, ca:cb, lo - 2 - cfg.HALO:hi - 2 - cfg.HALO],
                                  lambda ca, cb, lo=lo, hi=hi, t0=t0: xT[:, ca:cb, lo - t0:hi - t0], False)
        P.wait_all(SP, out_toks)
        P.emit()
    return nc


def prepare_inputs(cfg, x, mem, pre_norm_g, post_norm_g, mem_norm_g, w_mem_kv, w_out,
                   conv_w_in, conv_w, gmlp_w_in, gmlp_ln_g, gmlp_ln_b, gmlp_w_s, gmlp_b_s):
    D, KC, BC, MC, GG, L = cfg.D, cfg.KC, cfg.BC, cfg.MC, cfg.GG, cfg.DEPTH
    f32 = np.float32
    x = np.asarray(x, f32)[0]
    mem = np.asarray(mem, f32)[0]
    shared = {}
    shared["memT"] = np.ascontiguousarray(mem.T)
    cst = np.zeros((128, cfg.NCST), f32)

    def put(off, vec2d):
        n, W = vec2d.shape
        C = W // 128
        cst[:, off:off + n * C] = vec2d.reshape(n, C, 128).transpose(2, 0, 1).reshape(128, n * C)

    put(cfg.o_gpre, np.asarray(pre_norm_g, f32))
    put(cfg.o_gpost, np.asarray(post_norm_g, f32))
    put(cfg.o_gmem, np.asarray(mem_norm_g, f32))
    cw = np.asarray(conv_w, f32)
    put(cfg.o_cw, cw.reshape(cfg.NCONV * 3, cfg.BR))
    if cfg.NGM:
        put(cfg.o_lng, np.asarray(gmlp_ln_g, f32))
        put(cfg.o_lnb, np.asarray(gmlp_ln_b, f32))
    shared["cst"] = cst
    NBS = max(cfg.NGM, 1) * GG * 128
    cst2 = np.zeros((128, 2 * NBS + 256), f32)
    if cfg.NGM:
        bs = np.asarray(gmlp_b_s, f32).reshape(1, cfg.NGM * GG * 128)
        cst2[:, 0:NBS] = np.broadcast_to(bs, (128, NBS))
        ws = np.asarray(gmlp_w_s, f32)
        cst2[:, NBS:2 * NBS] = ws.transpose(3, 0, 1, 2).reshape(128, NBS)
    s_idx = np.arange(128)[:, None]
    t_idx = np.arange(128)[None, :]
    cst2[:, 2 * NBS:2 * NBS + 128] = (s_idx <= t_idx).astype(f32)
    cst2[:, 2 * NBS + 128:2 * NBS + 256] = np.eye(128, dtype=f32)
    shared["cst2"] = cst2
    for l in range(L):
        units = cfg.layer_units(l)
        chunks = [u[2] for u in units]
        Win = np.asarray(conv_w_in[l // 2] if l % 2 == 0 else gmlp_w_in[l // 2], f32)
        a = _units_from(Win, chunks, KC)
        b = _units_from(np.asarray(w_out[l], f32), list(range(KC)), KC)
        shared[f"w{l}"] = np.concatenate([a, b], axis=0)
    shared["wkv"] = np.concatenate([_units_from(np.asarray(w_mem_kv[l], f32), list(range(2 * MC)), KC) for l in range(L)], axis=0)
    in_maps = []
    pre = 2 + cfg.HALO
    for c in range(cfg.NCORES):
        s = c * cfg.TOWN
        xl = np.zeros((cfg.TLOC, D), f32)
        lo = s - pre
        if lo >= 0:
            xl[:] = x[lo:s + cfg.TOWN]
        else:
            xl[-lo:] = x[0:s + cfg.TOWN]
        m = dict(shared)
        m["xT"] = np.ascontiguousarray(xl.T)
        in_maps.append(m)
    return in_maps


_CACHE = {}


def kernel(x, mem, pre_norm_g, post_norm_g, mem_norm_g, w_mem_kv, w_out,
           conv_w_in, conv_w, gmlp_w_in, gmlp_ln_g, gmlp_ln_b, gmlp_w_s, gmlp_b_s):
    cfg = Cfg()
    in_maps = prepare_inputs(cfg, x, mem, pre_norm_g, post_norm_g, mem_norm_g, w_mem_kv, w_out,
                             conv_w_in, conv_w, gmlp_w_in, gmlp_ln_g, gmlp_ln_b, gmlp_w_s, gmlp_b_s)
    if "nc" not in _CACHE:
        _CACHE["nc"] = build_program(cfg)
    res = run_bass_kernel_spmd(_CACHE["nc"], in_maps, core_ids=list(range(cfg.NCORES)))
    outs = [np.asarray(r["outT"]).T for r in res.results]
    return np.ascontiguousarray(np.concatenate(outs, axis=0)[None]).astype(np.float32)
```
